# Optimizing a Trainium2 kernel written in Bass

```python
import math
import jax, jax.numpy as jnp
from jax import lax
import numpy as np

D_MODEL = 2048
BATCH = 4
SEQ = 4096
DEPTH = 2

N_MIXERS = 2
N_HEADS = 8
HEAD_DIM_QK = 128
HEAD_DIM_V = 256
Q_BLOCK = 128
D_RNN = D_MODEL
N_RNN_BLOCKS = 8
RNN_BLOCK = D_RNN // N_RNN_BLOCKS
CONV_WIDTH = 4
RG_C = 8.0
D_FF = 4 * D_MODEL
D_PLE = 256
N_ATTN = (DEPTH + 1) // 2
N_REC = DEPTH // 2
EPS = 1e-6

kernel_name = 'hybrid_diffattn_rglru_sqrelu_ple'


def rmsnorm(x, g):
    xf = x.astype(jnp.float32)
    y = xf * lax.rsqrt(jnp.mean(xf * xf, axis=-1, keepdims=True) + EPS)
    return (y * g.astype(jnp.float32)).astype(x.dtype)


def lambda_init(layer_idx):
    return 0.8 - 0.6 * math.exp(-0.3 * layer_idx)


def diff_attention(xn, w_qkv, g_q, g_k, lam_q1, lam_k1, lam_q2, lam_k2, g_sub, w_o, lam0):
    B, S, _ = xn.shape
    nq = N_HEADS * 2 * HEAD_DIM_QK
    qkv = xn @ w_qkv
    q, k, v = jnp.split(qkv, [nq, 2 * nq], axis=-1)
    q = rmsnorm(q.reshape(B, S, N_HEADS, 2, HEAD_DIM_QK), g_q)
    k = rmsnorm(k.reshape(B, S, N_HEADS, 2, HEAD_DIM_QK), g_k)
    v = v.reshape(B, S, N_HEADS, HEAD_DIM_V)
    lam = (jnp.exp(jnp.sum(lam_q1.astype(jnp.float32) * lam_k1.astype(jnp.float32)))
           - jnp.exp(jnp.sum(lam_q2.astype(jnp.float32) * lam_k2.astype(jnp.float32)))
           + lam0)
    slopes = jnp.exp2(-8.0 * jnp.arange(1, N_HEADS + 1, dtype=jnp.float32) / N_HEADS)
    scale = HEAD_DIM_QK ** -0.5
    n_blk = S // Q_BLOCK
    qb = q.reshape(B, n_blk, Q_BLOCK, N_HEADS, 2, HEAD_DIM_QK).transpose(1, 0, 2, 3, 4, 5)
    k_pos = jnp.arange(S)

    def block(args):
        q_blk, start = args
        s = jnp.einsum('bqhcd,bkhcd->bhcqk', q_blk, k,
                       preferred_element_type=jnp.float32) * scale
        q_pos = start + jnp.arange(Q_BLOCK)
        dist = (q_pos[:, None] - k_pos[None, :]).astype(jnp.float32)
        s = s - slopes[None, :, None, None, None] * dist
        s = jnp.where(dist >= 0, s, -jnp.inf)
        pr = jax.nn.softmax(s, axis=-1)
        attn = pr[:, :, 0] - lam * pr[:, :, 1]
        return jnp.einsum('bhqk,bkhd->bqhd', attn, v,
                          preferred_element_type=jnp.float32)

    o = lax.map(block, (qb, jnp.arange(n_blk) * Q_BLOCK))
    o = o.transpose(1, 0, 2, 3, 4).reshape(B, S, N_HEADS, HEAD_DIM_V)
    o = rmsnorm(o, g_sub) * (1.0 - lam0)
    return o.reshape(B, S, N_HEADS * HEAD_DIM_V).astype(xn.dtype) @ w_o


def rglru_block(xn, w_in, conv_w, conv_b, w_ga, b_ga, w_gx, b_gx, lam, w_o):
    B, S, _ = xn.shape
    u = xn @ w_in
    gate, xr = jnp.split(u, 2, axis=-1)
    y = jax.nn.gelu(gate, approximate=True)
    xc = lax.conv_general_dilated(
        xr, conv_w[:, None, :].astype(xr.dtype), window_strides=(1,),
        padding=[(CONV_WIDTH - 1, 0)], dimension_numbers=('NWC', 'WIO', 'NWC'),
        feature_group_count=D_RNN) + conv_b
    xb = xc.reshape(B, S, N_RNN_BLOCKS, RNN_BLOCK)
    r = jax.nn.sigmoid((jnp.einsum('bsni,nij->bsnj', xb, w_ga).reshape(B, S, D_RNN)
                        + b_ga).astype(jnp.float32))
    ig = jax.nn.sigmoid((jnp.einsum('bsni,nij->bsnj', xb, w_gx).reshape(B, S, D_RNN)
                         + b_gx).astype(jnp.float32))
    log_a = -RG_C * r * jax.nn.softplus(-lam.astype(jnp.float32))
    a = jnp.exp(log_a)
    b = jnp.sqrt(-jnp.expm1(2.0 * log_a)) * (ig * xc.astype(jnp.float32))

    def combine(left, right):
        a1, b1 = left
        a2, b2 = right
        return a1 * a2, a2 * b1 + b2

    _, h = lax.associative_scan(combine, (a, b), axis=1)
    return (h.astype(xn.dtype) * y) @ w_o


def sqrelu_mlp(xn, w_up, w_down):
    return jnp.square(jax.nn.relu(xn @ w_up)) @ w_down


def setup_inputs(seed: int = 0) -> dict:
    key = jax.random.key(seed)
    ks = jax.random.split(key, 32)
    f32 = jnp.float32
    D, H, DK, DV = D_MODEL, N_HEADS, HEAD_DIM_QK, HEAD_DIM_V

    def nrm(k, shape, scale):
        return jax.random.normal(k, shape, f32) * scale

    def gain(k, shape):
        return 1.0 + 0.05 * jax.random.normal(k, shape, f32)

    a_c = jax.random.uniform(ks[20], (N_REC, D_RNN), f32, 0.9, 0.999)
    s = a_c ** (1.0 / RG_C)
    lam_rec = jnp.log(s) - jnp.log1p(-s)
    return {
        'x': jax.random.normal(ks[0], (BATCH, SEQ, D), f32),
        'p': jax.random.normal(ks[1], (DEPTH, BATCH, SEQ, D_PLE), f32),
        'g_mix': gain(ks[2], (DEPTH, D)),
        'g_mlp': gain(ks[3], (DEPTH, D)),
        'g_ple': gain(ks[4], (DEPTH, D)),
        'w_qkv': nrm(ks[5], (N_ATTN, D, 4 * H * DK + H * DV), D ** -0.5),
        'g_q': gain(ks[6], (N_ATTN, DK)),
        'g_k': gain(ks[7], (N_ATTN, DK)),
        'lam_q1': nrm(ks[8], (N_ATTN, DK), 0.1),
        'lam_k1': nrm(ks[9], (N_ATTN, DK), 0.1),
        'lam_q2': nrm(ks[10], (N_ATTN, DK), 0.1),
        'lam_k2': nrm(ks[11], (N_ATTN, DK), 0.1),
        'g_sub': gain(ks[12], (N_ATTN, DV)),
        'w_o_attn': nrm(ks[13], (N_ATTN, H * DV, D), (H * DV) ** -0.5),
        'w_in_rec': nrm(ks[14], (N_REC, D, 2 * D_RNN), D ** -0.5),
        'conv_w': nrm(ks[15], (N_REC, CONV_WIDTH, D_RNN), CONV_WIDTH ** -0.5),
        'conv_b': nrm(ks[16], (N_REC, D_RNN), 0.01),
        'w_gate_a': nrm(ks[17], (N_REC, N_RNN_BLOCKS, RNN_BLOCK, RNN_BLOCK), RNN_BLOCK ** -0.5),
        'b_gate_a': nrm(ks[18], (N_REC, D_RNN), 0.01),
        'w_gate_x': nrm(ks[19], (N_REC, N_RNN_BLOCKS, RNN_BLOCK, RNN_BLOCK), RNN_BLOCK ** -0.5),
        'b_gate_x': nrm(ks[21], (N_REC, D_RNN), 0.01),
        'lam_rec': lam_rec,
        'w_o_rec': nrm(ks[22], (N_REC, D_RNN, D), D_RNN ** -0.5),
        'w_up': nrm(ks[23], (DEPTH, D, D_FF), D ** -0.5),
        'w_down': nrm(ks[24], (DEPTH, D_FF, D), D_FF ** -0.5),
        'w_ple_proj': nrm(ks[25], (DEPTH, D_PLE, D), D_PLE ** -0.5),
        'w_ple_gate': nrm(ks[26], (DEPTH, D, D), D ** -0.5),
    }


def reference(x, p, g_mix, g_mlp, g_ple, w_qkv, g_q, g_k, lam_q1, lam_k1, lam_q2,
              lam_k2, g_sub, w_o_attn, w_in_rec, conv_w, conv_b, w_gate_a, b_gate_a,
              w_gate_x, b_gate_x, lam_rec, w_o_rec, w_up, w_down, w_ple_proj,
              w_ple_gate):
    h = x
    for i in range(DEPTH):
        xn = rmsnorm(h, g_mix[i])
        j = i // N_MIXERS
        if i % N_MIXERS == 0:
            mix = diff_attention(xn, w_qkv[j], g_q[j], g_k[j], lam_q1[j], lam_k1[j],
                                 lam_q2[j], lam_k2[j], g_sub[j], w_o_attn[j],
                                 lambda_init(i))
        else:
            mix = rglru_block(xn, w_in_rec[j], conv_w[j], conv_b[j], w_gate_a[j],
                              b_gate_a[j], w_gate_x[j], b_gate_x[j], lam_rec[j],
                              w_o_rec[j])
        h = h + mix
        h = h + sqrelu_mlp(rmsnorm(h, g_mlp[i]), w_up[i], w_down[i])
        ple_gate = jax.nn.sigmoid(rmsnorm(h, g_ple[i]) @ w_ple_gate[i])
        h = h + ple_gate * (p[i] @ w_ple_proj[i])
    return h
```

```python
import math
import numpy as np
import concourse.bass as bass
import concourse.mybir as mybir
from concourse.bass_utils import run_bass_kernel_spmd

F32 = mybir.dt.float32
BF16 = mybir.dt.bfloat16
AF = mybir.ActivationFunctionType
ALU = mybir.AluOpType
AX = mybir.AxisListType

PE, ACT, DVE, POOL, SP = "tensor", "scalar", "vector", "gpsimd", "sync"
ENGINES = [PE, ACT, DVE, POOL, SP]

D = 2048
NCH = 16
NHEAD = 8
DFF = 8192
EPS = 1e-6
T = 1024
TW = 512
NT = T // TW
SCALE = 128 ** -0.5
LAM0 = 0.8 - 0.6 * math.exp(-0.3 * 0)
NV = 232


class Op:
    __slots__ = ("eng", "fn", "reads", "writes", "is_dma", "deps", "signal",
                 "token", "idx", "dma_sem", "pre_waits", "is_cc")

    def __init__(self, eng, fn, reads, writes, is_dma):
        self.eng = eng
        self.fn = fn
        self.reads = reads
        self.writes = writes
        self.is_dma = is_dma
        self.deps = []
        self.signal = False
        self.token = None
        self.dma_sem = None
        self.pre_waits = []
        self.is_cc = False


class Rec:
    def __init__(self):
        self.calls = []

    def __getattr__(self, name):
        def m(*a, **k):
            self.calls.append((name, a, k))
            return self
        return m


def _replay(calls, eng):
    last = None
    for name, a, k in calls:
        last = getattr(eng, name)(*a, **k)
    return last


class Prog:
    def __init__(self, nc, n_dma_sems=32):
        self.nc = nc
        self.ops = []
        self.last_writer = {}
        self.readers = {}
        self.n_dma_sems = n_dma_sems

    def op(self, eng, fn, reads=(), writes=(), dma=False):
        rec = Rec()
        fn(rec)
        calls = rec.calls
        assert calls, "empty op"
        o = Op(eng, (lambda e, calls=calls: _replay(calls, e)), tuple(reads), tuple(writes), dma)
        o.idx = len(self.ops)
        deps = set()
        for k in o.reads:
            w = self.last_writer.get(k)
            if w is not None:
                deps.add(w)
        for k in o.writes:
            w = self.last_writer.get(k)
            if w is not None:
                deps.add(w)
            for r in self.readers.get(k, ()):
                deps.add(r)
        deps.discard(o.idx)
        o.deps = sorted(deps)
        for k in o.writes:
            self.last_writer[k] = o.idx
            self.readers[k] = []
        for k in o.reads:
            if k not in o.writes:
                self.readers.setdefault(k, []).append(o.idx)
        self.ops.append(o)
        return o

    def dma(self, eng, fn, reads=(), writes=()):
        return self.op(eng, fn, reads, writes, dma=True)

    def cc(self, eng, fn, reads=(), writes=()):
        o = self.op(eng, fn, reads, writes, dma=True)
        o.is_cc = True
        return o

    def emit(self, final_wait_keys=()):
        nc = self.nc
        ops = self.ops
        final_ops = set()
        for k in final_wait_keys:
            w = self.last_writer.get(k)
            if w is not None:
                final_ops.add(w)
        for o in ops:
            for d in o.deps:
                p = ops[d]
                if p.is_dma:
                    continue
                if p.eng == PE and o.eng == PE and not o.is_dma:
                    continue
                p.signal = True
        for d in final_ops:
            if not ops[d].is_dma:
                ops[d].signal = True
        eng_sem = {e: nc.alloc_semaphore("S_" + e) for e in ENGINES}
        dma_sems = [nc.alloc_semaphore("D_%d" % i) for i in range(self.n_dma_sems)]
        eng_cnt = {e: 0 for e in ENGINES}
        dma_cnt = [0] * self.n_dma_sems
        dma_last = [None] * self.n_dma_sems
        dma_rr = 0
        dma_rr_sw = 0
        cc_sem = None
        cc_cnt = 0
        for o in ops:
            if o.is_cc:
                cc_sem = nc.alloc_semaphore("CC%d" % cc_cnt)
                o.dma_sem = cc_sem
                o.token = (("C", cc_cnt), cc_sem, 1)
                cc_cnt += 1
            elif o.is_dma:
                half = self.n_dma_sems // 2
                if o.eng == POOL:
                    k = half + dma_rr_sw
                    dma_rr_sw = (dma_rr_sw + 1) % half
                else:
                    k = dma_rr
                    dma_rr = (dma_rr + 1) % half
                if dma_last[k] is not None:
                    o.pre_waits.append(ops[dma_last[k]].token)
                dma_cnt[k] += 16
                dma_last[k] = o.idx
                o.dma_sem = dma_sems[k]
                o.token = (("D", k), dma_sems[k], dma_cnt[k])
            elif o.signal:
                eng_cnt[o.eng] += 1
                o.token = (("E", o.eng), eng_sem[o.eng], eng_cnt[o.eng])
        waited = {e: {} for e in ENGINES}
        per_eng = {e: [] for e in ENGINES}
        for o in ops:
            m = {}
            toks = [ops[d].token for d in o.deps] + list(o.pre_waits)
            for t in toks:
                if t is None:
                    continue
                key, sem, val = t
                if key == ("E", PE) and o.eng == PE and not o.is_dma:
                    continue
                if waited[o.eng].get(key, 0) >= val:
                    continue
                waited[o.eng][key] = val
                if key not in m or m[key][1] < val:
                    m[key] = (sem, val)
            per_eng[o.eng].append((o, list(m.values())))
        final_tokens = [ops[d].token for d in sorted(final_ops)]

        def run_engine(ename, eng):
            for o, waits in per_eng[ename]:
                for sem, val in waits:
                    eng.wait_ge(sem, val)
                last = o.fn(eng)
                if o.is_cc:
                    last.then_inc(o.dma_sem, 1)
                elif o.is_dma:
                    last.then_inc(o.dma_sem, 16)
                elif o.signal:
                    last.then_inc(eng_sem[ename], 1)
            if ename == SP:
                for t in final_tokens:
                    eng.wait_ge(t[1], t[2])

        with nc.Block() as block:
            @block.tensor
            def _(e):
                run_engine(PE, e)

            @block.scalar
            def _(e):
                run_engine(ACT, e)

            @block.vector
            def _(e):
                run_engine(DVE, e)

            @block.gpsimd
            def _(e):
                run_engine(POOL, e)

            @block.sync
            def _(e):
                run_engine(SP, e)
        self.stats = dict(n_ops=len(ops), eng_cnt=dict(eng_cnt),
                          per_eng={e: len(v) for e, v in per_eng.items()})


def build(S, n_layers=2, S_ctx=0, n_cores=4):
    NPASS = S // T
    NCTX = S_ctx // T
    SK = S + S_ctx
    EXCH = S_ctx > 0
    nc = bass.Bass("TRN2", target_bir_lowering=False)

    def din(name, shape, dt=F32):
        return nc.dram_tensor(name, list(shape), dt, kind="ExternalInput").ap()

    xT = din("xT", [D, S])
    onescol_d = din("onescol", [128, 32])
    pT = din("pT", [2, 256, S])
    vecs_d = din("vecs", [128, NV])
    lam_d = din("lamb", [128, 512])
    alibi_d = din("alibi", [128, 256])
    tri_d = din("tri", [128, 128])
    w_qkv = din("w_qkv", [D, 6144])
    w_oa = din("w_o_attn", [D, D])
    w_in = din("w_in_rec", [D, 4096])
    w_ga = din("w_gate_a", [8, 256, 256])
    w_gx = din("w_gate_x", [8, 256, 256])
    w_or = din("w_o_rec", [D, D])
    w_up = din("w_up", [2, D, DFF])
    w_down = din("w_down", [2, DFF, D])
    w_pp = din("w_ple_proj", [2, 256, D])
    w_pg = din("w_ple_gate", [2, D, D])
    outT = nc.dram_tensor("outT", [D, S], F32, kind="ExternalOutput").ap()
    assert EXCH and S_ctx == S, "sequence-split mode only"
    alibic_d = din("alibic", [128, 256])
    Kc = [[nc.dram_tensor("Kc_%d_%d" % (p_, hh), [1024, T], BF16).ap() for hh in range(2)] for p_ in range(NPASS)]
    Vc = [[nc.dram_tensor("Vc_%d_%d" % (p_, hh), [T // 2, D], BF16).ap() for hh in range(2)] for p_ in range(NPASS)]
    KGc = [[nc.dram_tensor("KG_%d_%d" % (p_, hh), [2 * 1024, T], BF16).ap() for hh in range(2)] for p_ in range(NPASS)]
    VGc = [[nc.dram_tensor("VG_%d_%d" % (p_, hh), [2 * (T // 2), D], BF16).ap() for hh in range(2)] for p_ in range(NPASS)]
    Qscr = nc.dram_tensor("Qscr", [16 * 128, S], BF16).ap()
    if EXCH:
        Hscr = nc.dram_tensor("Hscr", [NPASS, D, T], F32).ap()
        cbounce = nc.dram_tensor("cbounce", [128, 80], F32).ap()
        cgath = nc.dram_tensor("cgath", [256, 80], F32).ap()

    arena = nc.alloc_sbuf_tensor("arena", [128, 16384], F32)
    arena_b = arena.bitcast(BF16)
    h = arena[:].rearrange("p (c n) -> p c n", c=NCH)
    kT = arena_b[:, 0:8192].rearrange("p (c n) -> p c n", c=2)
    Vx = arena_b[:, 8192:8192 + 32 * 258].rearrange("p (j d) -> p j d", d=258)
    qT = arena_b[:, 16448:16448 + 2048].rearrange("p (c n) -> p c n", c=2)
    Pb = [arena_b[:, 18496 + i * 512:18496 + (i + 1) * 512].rearrange("p (c n) -> p c n", c=2)
          for i in range(3)]
    onb = [arena_b[:, 20032 + i * 256:20032 + (i + 1) * 256] for i in range(2)]
    ot = [arena[:, 10272 + i * 256:10272 + (i + 1) * 256] for i in range(4)]

    xn = nc.alloc_sbuf_tensor("xn", [128, NCH, T], BF16)
    act2 = nc.alloc_sbuf_tensor("act2", [128, NCH, T], BF16)
    act2f = act2[:].rearrange("p c n -> p (c n)")
    slabs = [nc.alloc_sbuf_tensor("slab%d" % i, [128, 8192], BF16) for i in range(3)]
    sq_all = nc.alloc_sbuf_tensor("sq_all", [128, 4 * TW], BF16)
    sq = [sq_all[:, i * TW:(i + 1) * TW] for i in range(4)]
    sq_f = sq_all.bitcast(F32)
    sqf = [sq_f[:, i * TW:(i + 1) * TW] for i in range(2)]
    rstd = [nc.alloc_sbuf_tensor("rstd%d" % i, [128, TW], F32) for i in range(2)]
    ybuf = [nc.alloc_sbuf_tensor("ybuf%d" % i, [128, TW], BF16) for i in range(2)]
    xrb = [nc.alloc_sbuf_tensor("xrb%d" % i, [128, TW + 8], F32) for i in range(2)]
    xcf = [nc.alloc_sbuf_tensor("xcf%d" % i, [128, TW], F32) for i in range(2)]
    xcb = [nc.alloc_sbuf_tensor("xcb%d" % i, [128, TW], BF16) for i in range(2)]
    rbuf = nc.alloc_sbuf_tensor("rbuf", [128, TW], F32)
    igbuf = nc.alloc_sbuf_tensor("igbuf", [128, TW], F32)
    sm = nc.alloc_sbuf_tensor("sm", [128, 32], F32)
    onescol = nc.alloc_sbuf_tensor("onescol_sb", [128, 32], F32)
    carry = nc.alloc_sbuf_tensor("carry", [128, 80], F32)
    carry_in = nc.alloc_sbuf_tensor("carry_in", [128, 80], F32)
    hist = nc.alloc_sbuf_tensor("hist", [128, NCH, 4], F32)
    state = nc.alloc_sbuf_tensor("state", [128, NCH], F32)
    c8 = nc.alloc_sbuf_tensor("c8", [128, NCH], F32)
    c8h = nc.alloc_sbuf_tensor("c8h", [128, NCH], F32)
    eps_t = nc.alloc_sbuf_tensor("eps_t", [128, 2], F32)
    gsc = nc.alloc_sbuf_tensor("gsc", [128, 2], F32)
    hb = nc.alloc_sbuf_tensor("hb", [128, 2 * NCH], F32)
    neglam = nc.alloc_sbuf_tensor("neglam", [128, 8], F32)
    vecs = nc.alloc_sbuf_tensor("vecs_sb", [128, NV], F32)
    alibi = nc.alloc_sbuf_tensor("alibi_sb", [128, 256], F32)
    alibic = nc.alloc_sbuf_tensor("alibic_sb", [128, 256], F32)
    lam_sb = nc.alloc_sbuf_tensor("lam_sb", [128, 512], F32)
    tri = nc.alloc_sbuf_tensor("tri_sb", [128, 128], BF16)
    ident = nc.alloc_sbuf_tensor("ident", [128, 128], BF16)
    ones = nc.alloc_sbuf_tensor("ones", [128, 128], BF16)

    ps = [nc.alloc_psum_tensor("ps%d" % i, [128, 512], F32) for i in range(7)]
    pst = nc.alloc_psum_tensor("pst", [128, 1024], BF16)

    P = Prog(nc)

    G_MIX = [0, 48]
    G_MLP = [16, 64]
    G_PLE = [32, 80]
    CONVW = 96
    CONVB = 160
    BGA = 176
    BGX = 192
    LAMR = 208
    GQ = 224
    GK = 225

    st = dict(gp=0, slab=0, sq=0, rstd=0, stage=0, dq=0)
    slab_gen = [0, 0, 0]

    def gp_bank():
        b = st["gp"]
        st["gp"] = (b + 1) % 7
        return b

    def next_sq():
        i = st["sq"]
        st["sq"] = (i + 1) % 4
        return i

    def next_rstd():
        i = st["rstd"]
        st["rstd"] = (i + 1) % 2
        return i

    def next_stage():
        i = st["stage"]
        st["stage"] = (i + 1) % 16
        return i

    def hk(c, tt):
        return ("h", c, tt)

    def xk(c, tt):
        return ("xn", c, tt)

    def ak(c, tt):
        return ("act2", c, tt)

    def tok(tt):
        return slice(tt * TW, (tt + 1) * TW)

    def slab_load(parts):
        s = st["slab"]
        st["slab"] = (s + 1) % 3
        slab_gen[s] += 1
        keys = []
        for i, (dv, src) in enumerate(parts):
            k = ("slab", s, i)
            keys.append(k)
            dst = dv(slabs[s])
            P.dma(POOL, lambda e, dst=dst, src=src: e.dma_start(out=dst, in_=src), writes=[k])
        return dict(s=s, gen=slab_gen[s], keys=keys, t=slabs[s])

    def use(hd):
        assert slab_gen[hd["s"]] == hd["gen"], "slab slot reclaimed while live"
        return hd["keys"]

    def col_slab(W, c0, w, nk=16):
        src = W.rearrange("(kc p) n -> p kc n", p=128)
        half = nk // 2
        parts = []
        for i in range(2):
            k0, k1 = i * half, (i + 1) * half
            parts.append((lambda t, k0=k0, k1=k1: t[:, 0:nk * w].rearrange("p (k n) -> p k n", k=nk)[:, k0:k1, :],
                          src[:, k0:k1, c0:c0 + w]))
        hd = slab_load(parts)
        hd["v"] = hd["t"][:, 0:nk * w].rearrange("p (k n) -> p k n", k=nk)
        return hd

    def row_slab(W, r0, nk, ncols):
        src = W[r0:r0 + nk * 128, :].rearrange("(kc p) n -> p kc n", p=128)
        half = max(1, nk // 2)
        parts = []
        for i in range(nk // half):
            k0, k1 = i * half, (i + 1) * half
            parts.append((lambda t, k0=k0, k1=k1: t[:, 0:nk * ncols].rearrange("p (k n) -> p k n", k=nk)[:, k0:k1, :],
                          src[:, k0:k1, :]))
        hd = slab_load(parts)
        hd["v"] = hd["t"][:, 0:nk * ncols].rearrange("p (k n) -> p k n", k=nk)
        return hd

    def mm_group(out_ap, bank, lhs_fn, rhs_fn, nk, reads):
        def f(e):
            last = None
            for kc in range(nk):
                last = e.matmul(out_ap, lhsT=lhs_fn(kc), rhs=rhs_fn(kc), start=(kc == 0), stop=(kc == nk - 1))
            return last
        P.op(PE, f, reads=reads, writes=[("ps", bank)])

    P.dma(SP, lambda e: e.dma_start(out=vecs[:], in_=vecs_d), writes=["vecs"])
    P.dma(SP, lambda e: e.dma_start(out=lam_sb[:], in_=lam_d), writes=["lam_sb"])
    P.dma(SP, lambda e: e.dma_start(out=alibi[:], in_=alibi_d), writes=["alibi"])
    P.dma(SP, lambda e: e.dma_start(out=alibic[:], in_=alibic_d), writes=["alibic"])
    P.dma(POOL, lambda e: e.dma_start(out=tri[:], in_=tri_d), writes=["tri"])
    P.dma(SP, lambda e: e.dma_start(out=onescol[:], in_=onescol_d), writes=["onescol"])

    P.op(POOL, lambda e: e.memset(ident[:], 1.0), writes=["ident"])
    P.op(POOL, lambda e: e.affine_select(out=ident[:], in_=ident[:], pattern=[[-1, 128]], compare_op=ALU.is_equal,
                                         fill=0.0, base=0, channel_multiplier=1), reads=["ident"], writes=["ident"])
    P.op(DVE, lambda e: e.memset(ones[:], 1.0), writes=["ones"])
    P.op(DVE, lambda e: e.memset(eps_t[:], float(EPS)), writes=["eps"])
    P.op(DVE, lambda e: e.tensor_scalar_mul(out=gsc[:], in0=vecs[:, 226:228], scalar1=float(1.0 - LAM0)),
         reads=["vecs"], writes=["gsc"])
    P.op(DVE, lambda e: e.memset(hist[:], 0.0), writes=[("hist", cc) for cc in range(NCH)])
    P.op(DVE, lambda e: e.memset(state[:], 0.0), writes=[("state", cc) for cc in range(NCH)])
    P.op(DVE, lambda e: e.tensor_tensor(out=rbuf[:, 0:128], in0=lam_sb[:, 0:128], in1=lam_sb[:, 128:256], op=ALU.mult),
         reads=["lam_sb"], writes=["rbuf"])
    P.op(DVE, lambda e: e.tensor_tensor(out=igbuf[:, 0:128], in0=lam_sb[:, 256:384], in1=lam_sb[:, 384:512], op=ALU.mult),
         reads=["lam_sb"], writes=["igbuf"])
    P.op(DVE, lambda e: e.reduce_sum(out=neglam[:, 0:1], in_=rbuf[:, 0:128], axis=AX.X),
         reads=["rbuf"], writes=["nl0"])
    P.op(DVE, lambda e: e.reduce_sum(out=neglam[:, 1:2], in_=igbuf[:, 0:128], axis=AX.X),
         reads=["igbuf"], writes=["nl01"])
    P.op(ACT, lambda e: e.activation(out=neglam[:, 2:4], in_=neglam[:, 0:2], func=AF.Exp), reads=["nl0", "nl01"], writes=["nl23"])
    P.op(DVE, lambda e: e.tensor_tensor(out=neglam[:, 4:5], in0=neglam[:, 3:4], in1=neglam[:, 2:3], op=ALU.subtract),
         reads=["nl23"], writes=["nl4"])
    P.op(DVE, lambda e: e.tensor_scalar_add(out=neglam[:, 5:6], in0=neglam[:, 4:5], scalar1=float(-LAM0)),
         reads=["nl4"], writes=["neglam"])
    NEGLAM = neglam[:, 5:6]
    P.op(ACT, lambda e: e.activation(out=c8[:], in_=vecs[:, LAMR:LAMR + 16], func=AF.Exp, scale=-1.0),
         reads=["vecs"], writes=["c8a"])
    P.op(ACT, lambda e: e.activation(out=c8[:], in_=c8[:], func=AF.Ln, bias=1.0), reads=["c8a"], writes=["c8b"])
    P.op(DVE, lambda e: e.tensor_scalar_mul(out=c8[:], in0=c8[:], scalar1=-8.0), reads=["c8b"], writes=["c8"])
    P.op(DVE, lambda e: e.tensor_scalar_mul(out=c8h[:], in0=c8[:], scalar1=0.5), reads=["c8"], writes=["c8h"])
    P.op(DVE, lambda e: e.tensor_scalar_mul(out=hb[:], in0=vecs[:, BGA:BGA + 2 * NCH], scalar1=0.5), reads=["vecs"], writes=["hb"])

    KT_ALL = [("kT", c, w, pp) for c in range(2) for w in ("ctx", "own") for pp in range(NPASS)]
    VX_ALL = [("Vx", w, pp, vh) for w in ("ctx", "own") for pp in range(NPASS) for vh in range(2)]
    ARENA_ATT_KEYS = KT_ALL + VX_ALL + [("qT", 0, 0), ("qT", 0, 1), ("qT", 1, 0), ("qT", 1, 1)] + \
        [("Pb", i) for i in range(3)] + [("onb", i) for i in range(2)] + [("ot", i) for i in range(4)]
    H_KEYS = [hk(c, tt) for c in range(NCH) for tt in range(NT)]

    def rms_stat(src_ap_fn, src_keys_fn, nsrc, inv_n, sq_eng=None):
        bank = gp_bank()
        for c in range(nsrc):
            si = next_sq()
            if sq_eng is not None and sq_eng(c) == DVE:
                P.op(DVE, lambda e, c=c, si=si: e.tensor_tensor(out=sq[si][:], in0=src_ap_fn(c), in1=src_ap_fn(c), op=ALU.mult),
                     reads=src_keys_fn(c), writes=[("sq", si)])
            else:
                P.op(ACT, lambda e, c=c, si=si: e.activation(out=sq[si][:], in_=src_ap_fn(c), func=AF.Square),
                     reads=src_keys_fn(c), writes=[("sq", si)])
            P.op(PE, lambda e, c=c, si=si: e.matmul(ps[bank][:], lhsT=ones[:], rhs=sq[si][:],
                                                     start=(c == 0), stop=(c == nsrc - 1)),
                 reads=[("sq", si), "ones"], writes=[("ps", bank)])
        ri = next_rstd()
        P.op(ACT, lambda e: e.activation(out=rstd[ri][:], in_=ps[bank][:], func=AF.Sqrt, bias=eps_t[:, 0:1], scale=float(inv_n)),
             reads=[("ps", bank), "eps"], writes=[("rstd", ri)])
        P.op(DVE, lambda e: e.reciprocal(out=rstd[ri][:], in_=rstd[ri][:]),
             reads=[("rstd", ri)], writes=[("rstd", ri)])
        return ri

    def rmsnorm_h(gcol):
        for tt in range(NT):
            ri = rms_stat(lambda c: h[:, c, tok(tt)], lambda c: [hk(c, tt)], NCH, 1.0 / D,
                          sq_eng=(lambda c: DVE if c % 2 else ACT) if tt == 0 else None)
            for c in range(NCH):
                P.op(DVE, lambda e, c=c: e.scalar_tensor_tensor(out=xn[:, c, tok(tt)], in0=h[:, c, tok(tt)],
                                                                scalar=vecs[:, gcol + c:gcol + c + 1], in1=rstd[ri][:],
                                                                op0=ALU.mult, op1=ALU.mult),
                     reads=[hk(c, tt), ("rstd", ri), "vecs"], writes=[xk(c, tt)])

    def load_h(pos, src=None, reads=()):
        src = xT if src is None else src
        for i in range(4):
            P.dma(SP, lambda e, i=i: e.dma_start(out=h[:, 4 * i:4 * i + 4, :],
                                                 in_=src.rearrange("(c p) n -> p c n", p=128)[:, 4 * i:4 * i + 4, pos:pos + T]),
                  reads=list(reads), writes=[hk(c, tt) for c in range(4 * i, 4 * i + 4) for tt in range(NT)])

    def qk_norm_evac(bank, gcolumn, out_ap, out_keys, defer=None):
        si = next_sq()
        P.op(ACT, lambda e: e.activation(out=sq[si][:], in_=ps[bank][:], func=AF.Square),
             reads=[("ps", bank)], writes=[("sq", si)])
        if defer is not None:
            defer.append(lambda: _qk_norm_tail(bank, si, gcolumn, out_ap, out_keys))
        else:
            _qk_norm_tail(bank, si, gcolumn, out_ap, out_keys)

    def _qk_norm_tail(bank, si, gcolumn, out_ap, out_keys):
        b2 = gp_bank()
        P.op(PE, lambda e: e.matmul(ps[b2][:], lhsT=ones[:], rhs=sq[si][:], start=True, stop=True),
             reads=[("sq", si), "ones"], writes=[("ps", b2)])
        ri = next_rstd()
        P.op(ACT, lambda e: e.activation(out=rstd[ri][:], in_=ps[b2][:], func=AF.Sqrt, bias=eps_t[:, 0:1], scale=1.0 / 128),
             reads=[("ps", b2), "eps"], writes=[("rstd", ri)])
        P.op(DVE, lambda e: e.reciprocal(out=rstd[ri][:], in_=rstd[ri][:]),
             reads=[("rstd", ri)], writes=[("rstd", ri)])
        P.op(DVE, lambda e: e.scalar_tensor_tensor(out=out_ap, in0=ps[bank][:], scalar=vecs[:, gcolumn:gcolumn + 1],
                                                   in1=rstd[ri][:], op0=ALU.mult, op1=ALU.mult),
             reads=[("ps", bank), ("rstd", ri), "vecs"], writes=out_keys)

    def qkv_phase(p, rgp):
        pos = p * T
        xkeys = lambda tt: [xk(c, tt) for c in range(NCH)]

        def gather(which):
            for hh in range(2):
                if which == "K":
                    P.cc(POOL, lambda e, hh=hh: e.collective_compute("AllGather", ALU.bypass, replica_groups=rgp,
                                                                     ins=[Kc[p][hh]], outs=[KGc[p][hh]]),
                         reads=[("Kc", idx, p, tt) for idx in range(8 * hh, 8 * hh + 8) for tt in range(NT)],
                         writes=[("KG", p, hh)])
                else:
                    P.cc(POOL, lambda e, hh=hh: e.collective_compute("AllGather", ALU.bypass, replica_groups=rgp,
                                                                     ins=[Vc[p][hh]], outs=[VGc[p][hh]]),
                         reads=[("Vc", s_, p, tb) for s_ in range(4) for tb in range(4 * hh, 4 * hh + 4)],
                         writes=[("VG", p, hh)])

        def qk_part(c0, gcol, kname, after_first=None):
            tails = []
            for s_ in range(4):
                sl = col_slab(w_qkv, c0 + 512 * s_, 512)
                for m in range(4):
                    idx = 4 * s_ + m
                    for tt in range(NT):
                        bank = gp_bank()
                        mm_group(ps[bank][:], bank, lambda kc: sl["v"][:, kc, m * 128:(m + 1) * 128],
                                 lambda kc: xn[:, kc, tok(tt)], NCH, use(sl) + xkeys(tt))
                        sg = next_stage()
                        dst = act2[:, sg, 0:TW]
                        prev_tail = list(tails)
                        del tails[:]
                        qk_norm_evac(bank, gcol, dst, [ak(sg, 0)], defer=tails)
                        if kname == "Kc":
                            r0 = (idx % 8) * 128
                            out_ap = Kc[p][idx // 8][r0:r0 + 128, tt * TW:(tt + 1) * TW]
                        else:
                            out_ap = Qscr[idx * 128:(idx + 1) * 128, pos + tt * TW:pos + (tt + 1) * TW]
                        tails.append(lambda dst=dst, out_ap=out_ap, sg=sg, idx=idx, tt=tt: P.dma(
                            SP, lambda e: e.dma_start(out=out_ap, in_=dst),
                            reads=[ak(sg, 0)], writes=[(kname, idx, p, tt)]))
                        for f_ in prev_tail:
                            f_()
                if s_ == 1 and after_first is not None:
                    after_first()
            for f_ in tails:
                f_()
            del tails[:]

        def v_part(after_first=None):
            for s_ in range(4):
                sl = col_slab(w_qkv, 4096 + 512 * s_, 512)
                for tb in range(T // 128):
                    bank = gp_bank()
                    mm_group(ps[bank][:], bank, lambda kc: xn[:, kc, tb * 128:(tb + 1) * 128],
                             lambda kc: sl["v"][:, kc, :], NCH, use(sl) + xkeys(tb // 4))
                    sg = next_stage()
                    dst = act2[:, sg, 0:TW]
                    P.op(ACT, lambda e, dst=dst, bank=bank: e.activation(out=dst, in_=ps[bank][:], func=AF.Copy),
                         reads=[("ps", bank)], writes=[ak(sg, 0)])
                    vr0 = (tb % 4) * 128
                    P.dma(SP, lambda e, dst=dst, tb=tb, s_=s_, vr0=vr0: e.dma_start(
                        out=Vc[p][tb // 4][vr0:vr0 + 128, s_ * 512:(s_ + 1) * 512], in_=dst),
                        reads=[ak(sg, 0)], writes=[("Vc", s_, p, tb)])
                if s_ == 1 and after_first is not None:
                    after_first()

        qk_part(2048, GK, "Kc")
        v_part(after_first=lambda: gather("K"))
        qk_part(0, GQ, "Qscr", after_first=lambda: gather("V"))

    def att_loads(p_own, hd):
        nown = (p_own + 1) * T
        ncb = S_ctx // 128
        for c in range(2):
            idx = 2 * hd + c
            kh, r0 = idx // 8, (idx % 8) * 128
            for pp in range(NCTX):
                P.dma(SP, lambda e, c=c, pp=pp: e.dma_start(out=kT[:, c, pp * T:(pp + 1) * T], in_=KGc[pp][kh][r0:r0 + 128, :]),
                      reads=[("KG", pp, kh)], writes=[("kT", c, "ctx", pp)])
            for pp in range(p_own + 1):
                P.dma(SP, lambda e, c=c, pp=pp: e.dma_start(out=kT[:, c, S_ctx + pp * T:S_ctx + (pp + 1) * T],
                                                            in_=Kc[pp][kh][r0:r0 + 128, :]),
                      reads=[("Kc", idx, pp, tt) for tt in range(NT)], writes=[("kT", c, "own", pp)])
            P.dma(SP, lambda e, c=c: e.dma_start(out=qT[:, c, :], in_=Qscr[idx * 128:(idx + 1) * 128, p_own * T:(p_own + 1) * T]),
                  reads=[("Qscr", idx, p_own, tt) for tt in range(NT)],
                  writes=[("qT", c, 0), ("qT", c, 1)])
        for pp in range(NCTX):
            for vh in range(2):
                b0 = pp * 8 + vh * 4
                P.dma(SP, lambda e, pp=pp, vh=vh, b0=b0: e.dma_start(
                    out=Vx[:, b0:b0 + 4, 0:256],
                    in_=VGc[pp][vh][0:T // 2, hd * 256:(hd + 1) * 256].rearrange("(j q) d -> q j d", q=128)),
                    reads=[("VG", pp, vh)], writes=[("Vx", "ctx", pp, vh)])
        for pp in range(p_own + 1):
            for vh in range(2):
                b0 = ncb + pp * 8 + vh * 4
                P.dma(SP, lambda e, pp=pp, vh=vh, b0=b0: e.dma_start(
                    out=Vx[:, b0:b0 + 4, 0:256],
                    in_=Vc[pp][vh][:, hd * 256:(hd + 1) * 256].rearrange("(j q) d -> q j d", q=128)),
                    reads=[("Vc", hd // 2, pp, tb) for tb in range(4 * vh, 4 * vh + 4)], writes=[("Vx", "own", pp, vh)])

    def att_prefetch(p_own):
        P.op(DVE, lambda e: e.tensor_copy(out=Vx[:, :, 256:257], in_=onescol[:, :].rearrange("p (j o) -> p j o", o=1)),
             reads=["onescol"], writes=H_KEYS + ARENA_ATT_KEYS)
        att_loads(p_own, 0)

    def attention(p_own, prefetched=False):
        p = p_own + NCTX
        pos = p * T
        nk = (p + 1) * T
        nown = (p_own + 1) * T
        ncb = S_ctx // 128
        if not prefetched:
            att_prefetch(p_own)
        for hd in range(NHEAD):
            if hd > 0:
                att_loads(p_own, hd)
            def kpiece(j):
                w, jj = ("ctx", j) if j < ncb else ("own", j - ncb)
                return [("kT", 0, w, jj // 8), ("kT", 1, w, jj // 8)]

            def vpiece(j):
                w, jj = ("ctx", j) if j < ncb else ("own", j - ncb)
                return [("Vx", w, jj // 8, (jj % 8) // 4)]
            acc = [[0, 1], [2, 3]]
            pending = []

            def flush_pending():
                while pending:
                    pending.pop(0)()

            def _part1(qb, hd, q0l, tt):
                a0, a1 = acc[qb]
                o0 = ot[qb]
                ob = onb[qb]
                sc = 8 * qb
                kk = lambda i, qb=qb: ("sm", qb, i)
                P.op(DVE, lambda e: e.reciprocal(out=sm[:, sc:sc + 1], in_=ps[a0][:, 256:257]),
                     reads=[("ps", a0)], writes=[kk(0)])
                P.op(DVE, lambda e: e.reciprocal(out=sm[:, sc + 1:sc + 2], in_=ps[a1][:, 256:257]),
                     reads=[("ps", a1)], writes=[kk(1)])
                P.op(DVE, lambda e: e.tensor_tensor(out=sm[:, sc + 2:sc + 3], in0=sm[:, sc + 1:sc + 2], in1=NEGLAM,
                                                    op=ALU.mult), reads=[kk(1), "neglam"], writes=[kk(2)])
                P.op(DVE, lambda e: e.tensor_scalar(out=o0, in0=ps[a0][:, 0:256], scalar1=sm[:, sc:sc + 1], scalar2=None,
                                                    op0=ALU.mult), reads=[("ps", a0), kk(0)], writes=[("ot", qb)])
                P.op(DVE, lambda e: e.scalar_tensor_tensor(out=ob, in0=ps[a1][:, 0:256], scalar=sm[:, sc + 2:sc + 3], in1=o0,
                                                           op0=ALU.mult, op1=ALU.add),
                     reads=[("ps", a1), kk(2), ("ot", qb)], writes=[("onb", qb)])

            for t in range(T // 256):
                Q0 = pos + 256 * t
                q0l = 256 * t
                jb = Q0 // 128
                jlast = jb + 1
                tt = q0l // TW

                def qk(j):
                    m = jb - j
                    qa = 128 if m == -1 else 0
                    sb = 4 + (j % 3)
                    sv = ps[sb][:].rearrange("p (c n) -> p c n", c=2)

                    def f(e):
                        last = None
                        for c in range(2):
                            last = e.matmul(sv[:, c, qa:256], lhsT=kT[:, c, j * 128:(j + 1) * 128],
                                            rhs=qT[:, c, q0l + qa:q0l + 256], start=True, stop=True)
                        return last
                    P.op(PE, f, reads=kpiece(j) + [("qT", 0, tt), ("qT", 1, tt)], writes=[("ps", sb)])
                    pi = j % 3
                    col = hd * 32 + (m + 1)
                    atab = alibic if j < ncb else alibi
                    P.op(ACT, lambda e: e.activation(out=Pb[pi][:, :, qa:256], in_=sv[:, :, qa:256], func=AF.Exp,
                                                     bias=atab[:, col:col + 1], scale=float(SCALE)),
                         reads=[("ps", sb), "alibi", "alibic"], writes=[("Pb", pi)])
                    if m == 0 or m == -1:
                        lo = 0 if m == 0 else 128

                        def fm(e):
                            last = None
                            for c in range(2):
                                last = e.tensor_tensor(out=Pb[pi][:, c, lo:lo + 128], in0=Pb[pi][:, c, lo:lo + 128],
                                                       in1=tri[:], op=ALU.mult)
                            return last
                        P.op(DVE, fm, reads=[("Pb", pi), "tri"], writes=[("Pb", pi)])

                def pv(j):
                    m = jb - j
                    pi = j % 3
                    for qb in range(2):
                        if m == -1 and qb == 0:
                            continue
                        lastj = jb if qb == 0 else jb + 1

                        def f(e, qb=qb, lastj=lastj):
                            last = None
                            for c in range(2):
                                last = e.matmul(ps[acc[qb][c]][:, 0:257], lhsT=Pb[pi][:, c, qb * 128:(qb + 1) * 128],
                                                rhs=Vx[:, j, 0:257], start=(j == 0), stop=(j == lastj))
                            return last
                        P.op(PE, f, reads=[("Pb", pi)] + vpiece(j), writes=[("ps", acc[qb][0]), ("ps", acc[qb][1])])

                qk(0)
                qk(1)
                def part1(qb):
                    return _part1(qb, hd, q0l, tt)

                for j in range(jlast + 1):
                    if j + 2 <= jlast:
                        qk(j + 2)
                    pv(j)
                    if j == 2:
                        flush_pending()
                    if j == jb:
                        part1(0)
                flush_pending()
                part1(1)
                def part2(qb, q0l=q0l, tt=tt):
                    ob = onb[qb]
                    tsl = slice(q0l + qb * 128, q0l + (qb + 1) * 128)

                    def ftr(e):
                        last = None
                        for half in range(2):
                            last = e.transpose(pst[:, half * 128:(half + 1) * 128], ob[:, half * 128:(half + 1) * 128], ident[:])
                        return last
                    P.op(PE, ftr, reads=[("onb", qb), "ident"], writes=[("pst",)])
                    P.op(DVE, lambda e: e.tensor_copy(out=act2[:, 2 * hd:2 * hd + 2, tsl],
                                                      in_=pst[:, 0:256].rearrange("p (a b) -> p a b", a=2)),
                         reads=[("pst",)], writes=[ak(2 * hd, tt), ak(2 * hd + 1, tt)])
                pending.append(lambda: part2(0))
                pending.append(lambda: part2(1))
            flush_pending()
            for tt in range(NT):
                ri = rms_stat(lambda c: act2[:, 2 * hd + c, tok(tt)], lambda c: [ak(2 * hd + c, tt)], 2, 1.0 / 256)
                for c in range(2):
                    P.op(DVE, lambda e, c=c: e.scalar_tensor_tensor(out=act2[:, 2 * hd + c, tok(tt)], in0=act2[:, 2 * hd + c, tok(tt)],
                                                                    scalar=gsc[:, c:c + 1], in1=rstd[ri][:],
                                                                    op0=ALU.mult, op1=ALU.mult),
                         reads=[ak(2 * hd + c, tt), ("rstd", ri), "gsc"], writes=[ak(2 * hd + c, tt)])
        P.op(DVE, lambda e: e.memset(sm[:, 31:32], 0.0), reads=[], writes=H_KEYS + ARENA_ATT_KEYS)

    def proj_add_h(W, in_buf, in_key):
        for s in range(4):
            sl = col_slab(W, 512 * s, 512)
            order = [(m, tt) for m in range(4) for tt in range(NT)] if s < 3 else [(m, tt) for tt in range(NT) for m in range(4)]
            for (m, tt) in order:
                oc = 4 * s + m
                if True:
                    bank = gp_bank()
                    mm_group(ps[bank][:], bank, lambda kc: sl["v"][:, kc, m * 128:(m + 1) * 128],
                             lambda kc: in_buf[:, kc, tok(tt)], NCH, use(sl) + [in_key(kc, tt) for kc in range(NCH)])
                    P.op(DVE, lambda e, oc=oc, tt=tt, bank=bank: e.tensor_tensor(out=h[:, oc, tok(tt)], in0=h[:, oc, tok(tt)],
                                                                                 in1=ps[bank][:], op=ALU.add),
                         reads=[("ps", bank), hk(oc, tt)], writes=[hk(oc, tt)])

    def mlp(l):
        rmsnorm_h(G_MLP[l])
        NS = DFF // 512
        wu = [None] * NS
        wd = [None] * NS

        def up(s):
            sl = wu[s]
            hb = s % 2
            for m in range(4):
                for tt in range(NT):
                    bank = gp_bank()
                    mm_group(ps[bank][:], bank, lambda kc: sl["v"][:, kc, m * 128:(m + 1) * 128],
                             lambda kc: xn[:, kc, tok(tt)], NCH, use(sl) + [xk(kc, tt) for kc in range(NCH)])
                    dst = act2[:, 4 * hb + m, tok(tt)]
                    P.op(ACT, lambda e, dst=dst, bank=bank: e.activation(out=dst, in_=ps[bank][:], func=AF.Relu),
                         reads=[("ps", bank)], writes=[ak(4 * hb + m, tt)])
                    P.op(ACT, lambda e, dst=dst: e.activation(out=dst, in_=dst, func=AF.Square),
                         reads=[ak(4 * hb + m, tt)], writes=[ak(4 * hb + m, tt)])

        def down(s):
            sl = wd[s]
            hb = s % 2
            order = [(oc, tt) for oc in range(NCH) for tt in range(NT)] if s < NS - 1 else \
                [(oc, tt) for tt in range(NT) for oc in range(NCH)]
            for (oc, tt) in order:
                if True:
                    bank = gp_bank()
                    mm_group(ps[bank][:], bank, lambda kc: sl["v"][:, kc, oc * 128:(oc + 1) * 128],
                             lambda kc: act2[:, 4 * hb + kc, tok(tt)], 4, use(sl) + [ak(4 * hb + kc, tt) for kc in range(4)])
                    P.op(DVE, lambda e, oc=oc, tt=tt, bank=bank: e.tensor_tensor(out=h[:, oc, tok(tt)], in0=h[:, oc, tok(tt)],
                                                                                 in1=ps[bank][:], op=ALU.add),
                         reads=[("ps", bank), hk(oc, tt)], writes=[hk(oc, tt)])

        wu[0] = col_slab(w_up[l], 0, 512)
        up(0)
        for s in range(1, NS):
            wu[s] = col_slab(w_up[l], 512 * s, 512)
            wd[s - 1] = row_slab(w_down[l], 512 * (s - 1), 4, D)
            up(s)
            down(s - 1)
        wd[NS - 1] = row_slab(w_down[l], 512 * (NS - 1), 4, D)
        down(NS - 1)

    def ple(l, p):
        pos = p * T
        rmsnorm_h(G_PLE[l])
        pb = act2[:, 0:2, :]
        P.dma(POOL, lambda e: e.dma_start(out=pb, in_=pT[l].rearrange("(kc q) n -> q kc n", q=128)[:, :, pos:pos + T]),
              writes=[ak(c, tt) for c in range(2) for tt in range(NT)])
        wpp = act2f[:, 2048:2048 + 4096].rearrange("p (k n) -> p k n", k=2)
        P.dma(POOL, lambda e: e.dma_start(out=wpp, in_=w_pp[l].rearrange("(kc q) n -> q kc n", q=128)),
              writes=[ak(c, tt) for c in range(2, 6) for tt in range(NT)])
        wpp_keys = [ak(c, tt) for c in range(2, 6) for tt in range(NT)]
        tb = [rbuf, igbuf]
        tbk = ["rbuf", "igbuf"]
        n = 0
        for s in range(4):
            sl = col_slab(w_pg[l], 512 * s, 512)
            order = [(m, tt) for m in range(4) for tt in range(NT)] if s < 3 else [(m, tt) for tt in range(NT) for m in range(4)]
            for (m, tt) in order:
                oc = 4 * s + m
                if True:
                    bg = gp_bank()
                    mm_group(ps[bg][:], bg, lambda kc: sl["v"][:, kc, m * 128:(m + 1) * 128],
                             lambda kc: xn[:, kc, tok(tt)], NCH, use(sl) + [xk(kc, tt) for kc in range(NCH)])
                    bp = gp_bank()
                    mm_group(ps[bp][:], bp, lambda kc: wpp[:, kc, oc * 128:(oc + 1) * 128],
                             lambda kc: pb[:, kc, tok(tt)], 2, wpp_keys + [ak(0, tt), ak(1, tt)])
                    tbi = n % 2
                    n += 1
                    tbuf, tkey = tb[tbi], tbk[tbi]
                    P.op(ACT, lambda e, tbuf=tbuf, bg=bg: e.activation(out=tbuf[:], in_=ps[bg][:], func=AF.Sigmoid),
                         reads=[("ps", bg)], writes=[tkey])
                    P.op(DVE, lambda e, tbuf=tbuf, bp=bp: e.tensor_tensor(out=tbuf[:], in0=tbuf[:], in1=ps[bp][:], op=ALU.mult),
                         reads=[tkey, ("ps", bp)], writes=[tkey])
                    P.op(DVE, lambda e, tbuf=tbuf, oc=oc, tt=tt: e.tensor_tensor(out=h[:, oc, tok(tt)], in0=h[:, oc, tok(tt)],
                                                                                 in1=tbuf[:], op=ALU.add),
                         reads=[tkey, hk(oc, tt)], writes=[hk(oc, tt)])

    def rglru(lite=False):
        rb = [rbuf, rstd[0]]
        rbk = ["rbuf", ("rstd", 0)]
        ib = [igbuf, rstd[1]]
        ibk = ["igbuf", ("rstd", 1)]
        XC = [[xcf[0], xcf[1]], [sqf[0], sqf[1]]]
        XCK = [[[("xcf", 0)], [("xcf", 1)]], [[("sq", 0), ("sq", 1)], [("sq", 2), ("sq", 3)]]]
        XB = [[xcb[0], xcb[1]], [ybuf[0], ybuf[1]]]
        XBK = [[("xcb", 0), ("xcb", 1)], [("ybuf", 0), ("ybuf", 1)]]
        slabsG = {}

        def stageA(i, n, tt, sA):
            par = i % 2
            xkeys = [xk(kc, tt) for kc in range(NCH)]
            for j in range(2):
                cc = 2 * n + j
                xm, xh = ("xrb_main", j), ("xrb_hist", j)
                xc, xck = XC[par][j], XCK[par][j]
                if not lite:
                    bank = gp_bank()
                    mm_group(ps[bank][:], bank, lambda kc: sA["v"][:, kc, j * 128:(j + 1) * 128],
                             lambda kc: xn[:, kc, tok(tt)], NCH, use(sA) + xkeys)
                    P.op(ACT, lambda e: e.activation(out=act2[:, cc, tok(tt)], in_=ps[bank][:], func=AF.Gelu_apprx_tanh),
                         reads=[("ps", bank)], writes=[ak(cc, tt)])
                bank2 = gp_bank()
                mm_group(ps[bank2][:], bank2, lambda kc: sA["v"][:, kc, 256 + j * 128:256 + (j + 1) * 128],
                         lambda kc: xn[:, kc, tok(tt)], NCH, use(sA) + xkeys)
                P.op(ACT, lambda e: e.activation(out=xrb[j][:, 3:3 + TW], in_=ps[bank2][:], func=AF.Copy),
                     reads=[("ps", bank2)], writes=[xm])
                P.op(DVE, lambda e: e.tensor_copy(out=xrb[j][:, 0:3], in_=hist[:, cc, 0:3]),
                     reads=[("hist", cc)], writes=[xh])
                cw = lambda jj, cc=cc: vecs[:, CONVW + 16 * jj + cc:CONVW + 16 * jj + cc + 1]
                P.op(ACT, lambda e: e.activation(out=xc[:], in_=ps[bank2][:], func=AF.Identity,
                                                 bias=vecs[:, CONVB + cc:CONVB + cc + 1], scale=cw(3)),
                     reads=[("ps", bank2), "vecs"], writes=xck)
                for jj in (2, 1, 0):
                    P.op(DVE, lambda e, jj=jj: e.scalar_tensor_tensor(out=xc[:], in0=xrb[j][:, jj:jj + TW], scalar=cw(jj),
                                                                      in1=xc[:], op0=ALU.mult, op1=ALU.add),
                         reads=[xm, xh, "vecs"] + xck, writes=xck)
                P.op(DVE, lambda e: e.tensor_copy(out=hist[:, cc, 0:3], in_=xrb[j][:, TW:TW + 3]),
                     reads=[xm], writes=[("hist", cc)])

        def stageA2(i):
            par = i % 2
            for j in range(2):
                P.op(ACT, lambda e: e.activation(out=XB[par][j][:], in_=XC[par][j][:], func=AF.Copy),
                     reads=XCK[par][j], writes=[XBK[par][j]])

        def stageB(i, n, tt, sG):
            par = i % 2
            brs, bis = [], []
            for j in range(2):
                br = gp_bank()
                mm_group(ps[br][:], br, lambda kc: sG["v"][:, kc, j * 128:(j + 1) * 128],
                         lambda kc: XB[par][kc][:], 2, use(sG) + XBK[par])
                bi = gp_bank()
                mm_group(ps[bi][:], bi, lambda kc: sG["v"][:, kc, 256 + j * 128:256 + (j + 1) * 128],
                         lambda kc: XB[par][kc][:], 2, use(sG) + XBK[par])
                brs.append(br)
                bis.append(bi)
            for j in range(2):
                cc = 2 * n + j
                P.op(ACT, lambda e: e.activation(out=rb[j][:], in_=ps[brs[j]][:], func=AF.Tanh,
                                                 bias=hb[:, cc:cc + 1], scale=0.5),
                     reads=[("ps", brs[j]), "hb"], writes=[rbk[j]])
                P.op(ACT, lambda e: e.activation(out=ib[j][:], in_=ps[bis[j]][:], func=AF.Tanh,
                                                 bias=hb[:, NCH + cc:NCH + cc + 1], scale=0.5),
                     reads=[("ps", bis[j]), "hb"], writes=[ibk[j]])
            for j in range(2):
                cc = 2 * n + j
                xc, xck = XC[par][j], XCK[par][j]
                P.op(ACT, lambda e: e.activation(out=rb[j][:], in_=rb[j][:], func=AF.Exp,
                                                 bias=c8h[:, cc:cc + 1], scale=c8h[:, cc:cc + 1]),
                     reads=[rbk[j], "c8h"], writes=[rbk[j]])
                P.op(DVE, lambda e: e.scalar_tensor_tensor(out=ib[j][:], in0=ib[j][:], scalar=1.0, in1=xc[:],
                                                           op0=ALU.add, op1=ALU.mult),
                     reads=[ibk[j]] + xck, writes=[ibk[j]])
            for j in range(2):
                xc, xck = XC[par][j], XCK[par][j]
                P.op(ACT, lambda e: e.activation(out=xc[:], in_=rb[j][:], func=AF.Square),
                     reads=[rbk[j]], writes=xck)
            for j in range(2):
                xc, xck = XC[par][j], XCK[par][j]
                P.op(ACT, lambda e: e.activation(out=xc[:], in_=xc[:], func=AF.Sqrt, bias=0.25, scale=-0.25),
                     reads=xck, writes=xck)
            for j in range(2):
                cc = 2 * n + j
                xc, xck = XC[par][j], XCK[par][j]
                P.op(DVE, lambda e: e.tensor_tensor(out=ib[j][:], in0=ib[j][:], in1=xc[:], op=ALU.mult),
                     reads=[ibk[j]] + xck, writes=[ibk[j]])
                P.op(DVE, lambda e: e.tensor_tensor_scan(out=xc[:], data0=rb[j][:], data1=ib[j][:],
                                                         initial=state[:, cc:cc + 1], op0=ALU.mult, op1=ALU.add),
                     reads=[rbk[j], ibk[j], ("state", cc)] + xck, writes=xck)
                P.op(DVE, lambda e: e.tensor_copy(out=state[:, cc:cc + 1], in_=xc[:, TW - 1:TW]),
                     reads=xck, writes=[("state", cc)])
                if not lite:
                    P.op(DVE, lambda e: e.tensor_tensor(out=act2[:, cc, tok(tt)], in0=xc[:], in1=act2[:, cc, tok(tt)], op=ALU.mult),
                         reads=xck + [ak(cc, tt)], writes=[ak(cc, tt)])

        steps = [(n, tt) for n in range(8) for tt in range(NT)]
        prev = None
        sA = None
        for i, (n, tt) in enumerate(steps):
            if tt == 0:
                srcw = w_in.rearrange("(kc q) n -> q kc n", q=128)
                partsA = [
                    (lambda t: t[:, 0:8192].rearrange("p (k n) -> p k n", k=16)[:, :, 0:256], srcw[:, :, n * 256:(n + 1) * 256]),
                    (lambda t: t[:, 0:8192].rearrange("p (k n) -> p k n", k=16)[:, :, 256:512],
                     srcw[:, :, 2048 + n * 256:2048 + (n + 1) * 256]),
                ]
                if lite:
                    partsA = partsA[1:]
                sA = slab_load(partsA)
                sA["v"] = sA["t"][:, 0:8192].rearrange("p (k n) -> p k n", k=16)
                partsG = [
                    (lambda t: t[:, 0:1024].rearrange("p (k n) -> p k n", k=2)[:, :, 0:256],
                     w_ga[n].rearrange("(kc q) n -> q kc n", q=128)),
                    (lambda t: t[:, 0:1024].rearrange("p (k n) -> p k n", k=2)[:, :, 256:512],
                     w_gx[n].rearrange("(kc q) n -> q kc n", q=128)),
                ]
                sG = slab_load(partsG)
                sG["v"] = sG["t"][:, 0:1024].rearrange("p (k n) -> p k n", k=2)
                slabsG[n] = sG
            stageA(i, n, tt, sA)
            if prev is not None:
                stageB(prev[0], prev[1], prev[2], slabsG[prev[1]])
            stageA2(i)
            prev = (i, n, tt)
        stageB(prev[0], prev[1], prev[2], slabsG[prev[1]])

    out_keys = []

    def layer0(p_own):
        pos = p_own * T
        attention(p_own, prefetched=(p_own > 0))
        load_h(pos)
        proj_add_h(w_oa, act2, ak)
        mlp(0)
        ple(0, p_own)

    def layer1(p_own, norm_done=False):
        if not norm_done:
            rmsnorm_h(G_MIX[1])
        rglru()
        proj_add_h(w_or, act2, ak)
        mlp(1)
        ple(1, p_own)

    def store_out(p_own):
        pos = p_own * T
        for i in range(4):
            k = ("out", p_own, i)
            out_keys.append(k)
            P.dma(SP, lambda e, i=i, pos=pos: e.dma_start(
                out=outT.rearrange("(c q) n -> q c n", q=128)[:, 4 * i:4 * i + 4, pos:pos + T], in_=h[:, 4 * i:4 * i + 4, :]),
                reads=[hk(c, tt) for c in range(4 * i, 4 * i + 4) for tt in range(NT)], writes=[k])

    if True:
        rgp = [[2 * i, 2 * i + 1] for i in range(n_cores // 2)]
        load_h(0)
        for p in range(NPASS):
            rmsnorm_h(G_MIX[0])
            if p + 1 < NPASS:
                load_h((p + 1) * T)
            qkv_phase(p, rgp)
        for p in range(NPASS):
            layer0(p)
            for i in range(4):
                P.dma(SP, lambda e, i=i, p=p: e.dma_start(
                    out=Hscr[p].rearrange("(c q) n -> q c n", q=128)[:, 4 * i:4 * i + 4, :], in_=h[:, 4 * i:4 * i + 4, :]),
                    reads=[hk(c, tt) for c in range(4 * i, 4 * i + 4) for tt in range(NT)], writes=[("Hscr", p, i)])
            rmsnorm_h(G_MIX[1])
            if p == NPASS - 1:
                load_h(0, src=Hscr[0], reads=[("Hscr", 0, i) for i in range(4)])
            else:
                att_prefetch(p + 1)
            rglru(lite=True)
        P.op(DVE, lambda e: e.tensor_copy(out=carry[:, 0:16], in_=state[:]),
             reads=[("state", cc) for cc in range(NCH)], writes=["carry_a"])
        P.op(DVE, lambda e: e.tensor_copy(out=carry[:, 16:80], in_=hist[:].rearrange("p c k -> p (c k)")),
             reads=[("hist", cc) for cc in range(NCH)], writes=["carry_b"])
        P.dma(SP, lambda e: e.dma_start(out=cbounce, in_=carry[:]), reads=["carry_a", "carry_b"], writes=["cbounce"])
        rg = [[2 * i, 2 * i + 1] for i in range(n_cores // 2)]
        P.cc(POOL, lambda e: e.collective_compute("AllGather", ALU.bypass, replica_groups=rg, ins=[cbounce], outs=[cgath]),
             reads=["cbounce"], writes=["cgath"])
        rmsnorm_h(G_MIX[1])
        P.dma(SP, lambda e: e.dma_start(out=carry_in[:], in_=cgath[0:128, :]), reads=["cgath"], writes=["carry_in"])
        P.op(DVE, lambda e: e.tensor_scalar(out=carry_in[:], in0=carry_in[:], scalar1=vecs[:, 228:229], scalar2=None, op0=ALU.mult),
             reads=["carry_in", "vecs"], writes=["carry_in"])
        P.op(DVE, lambda e: e.tensor_copy(out=state[:], in_=carry_in[:, 0:16]),
             reads=["carry_in"], writes=[("state", cc) for cc in range(NCH)])
        P.op(DVE, lambda e: e.tensor_copy(out=hist[:].rearrange("p c k -> p (c k)"), in_=carry_in[:, 16:80]),
             reads=["carry_in"], writes=[("hist", cc) for cc in range(NCH)])
        for p in range(NPASS):
            if p > 0:
                load_h(0, src=Hscr[p], reads=[("Hscr", p, i) for i in range(4)])
            layer1(p, norm_done=(p == 0))
            store_out(p)
    P.emit(final_wait_keys=out_keys)
    return nc, P


def _fv(v):
    return np.ascontiguousarray(np.asarray(v, np.float32).reshape(NCH, 128).T)


def _consts():
    kl = np.arange(128, dtype=np.float32)[:, None]
    alibi = np.zeros((128, 256), np.float32)
    for hh in range(NHEAD):
        slope = 2.0 ** (-8.0 * (hh + 1) / NHEAD)
        for mi in range(32):
            alibi[:, hh * 32 + mi] = (slope * (kl[:, 0] - 128.0 * mi)).astype(np.float32)
    tri = (np.arange(128)[:, None] <= np.arange(128)[None, :]).astype(np.float32)
    return alibi, tri


def make_in_maps(S, batches, x, p, g_mix, g_mlp, g_ple, w_qkv, g_q, g_k, lam_q1, lam_k1, lam_q2,
                 lam_k2, g_sub, w_o_attn, w_in_rec, conv_w, conv_b, w_gate_a, b_gate_a,
                 w_gate_x, b_gate_x, lam_rec, w_o_rec, w_up, w_down, w_ple_proj, w_ple_gate, split=False):
    f = lambda a: np.ascontiguousarray(np.asarray(a, np.float32))
    vecs = np.zeros((128, NV), np.float32)
    vecs[:, 0:16] = _fv(g_mix[0]); vecs[:, 16:32] = _fv(g_mlp[0]); vecs[:, 32:48] = _fv(g_ple[0])
    vecs[:, 48:64] = _fv(g_mix[1]); vecs[:, 64:80] = _fv(g_mlp[1]); vecs[:, 80:96] = _fv(g_ple[1])
    for j in range(4):
        vecs[:, 96 + 16 * j:96 + 16 * (j + 1)] = _fv(conv_w[0, j])
    vecs[:, 160:176] = _fv(conv_b[0]); vecs[:, 176:192] = _fv(b_gate_a[0]); vecs[:, 192:208] = _fv(b_gate_x[0])
    vecs[:, 208:224] = _fv(lam_rec[0])
    vecs[:, 224] = np.asarray(g_q[0], np.float32); vecs[:, 225] = np.asarray(g_k[0], np.float32)
    vecs[:, 226:228] = np.asarray(g_sub[0], np.float32).reshape(2, 128).T
    lamv = np.concatenate([np.asarray(a[0], np.float32) for a in (lam_q1, lam_k1, lam_q2, lam_k2)])
    lamb = np.ascontiguousarray(np.broadcast_to(lamv[None, :], (128, 512)))
    alibi, tri = _consts()
    shared = {
        "vecs": vecs, "lamb": lamb, "alibi": alibi, "tri": tri,
        "w_qkv": f(w_qkv[0]), "w_o_attn": f(w_o_attn[0]), "w_in_rec": f(w_in_rec[0]),
        "w_gate_a": f(w_gate_a[0]), "w_gate_x": f(w_gate_x[0]), "w_o_rec": f(w_o_rec[0]),
        "w_up": f(w_up), "w_down": f(w_down), "w_ple_proj": f(w_ple_proj), "w_ple_gate": f(w_ple_gate),
    }
    maps = []
    for b in batches:
        for hf in range(2 if split else 1):
            m = dict(shared)
            t0 = hf * S
            m["xT"] = np.ascontiguousarray(np.asarray(x[b, t0:t0 + S], np.float32).T)
            m["pT"] = np.ascontiguousarray(np.asarray(p[:, b, t0:t0 + S], np.float32).transpose(0, 2, 1))
            oc = np.ones((128, 32), np.float32)
            vv = vecs.copy()
            vv[:, 228] = float(hf)
            m["vecs"] = vv
            m["alibic"] = alibi if (hf == 1 or not split) else (alibi - np.float32(30000.0)).astype(np.float32)
            m["onescol"] = oc
            maps.append(m)
    return maps


def kernel(**inputs):
    x = inputs["x"]
    B, S, _ = x.shape
    SH = S // 2
    n_cores = 2 * B
    nc, _P = build(SH, S_ctx=SH, n_cores=n_cores)
    maps = make_in_maps(SH, list(range(B)), split=True, **inputs)
    res = run_bass_kernel_spmd(nc, maps, core_ids=list(range(n_cores)))
    out = np.empty((B, S, D), np.float32)
    for b in range(B):
        for hf in range(2):
            out[b, hf * SH:(hf + 1) * SH] = np.asarray(res.results[2 * b + hf]["outT"], np.float32).T
    return out
```

```python
import math
import numpy as np
import concourse.bass as bass
import concourse.mybir as mybir
from concourse.bass_utils import run_bass_kernel_spmd

F32 = mybir.dt.float32
BF16 = mybir.dt.bfloat16
AF = mybir.ActivationFunctionType
ALU = mybir.AluOpType
AX = mybir.AxisListType

PE, ACT, DVE, POOL, SP = "tensor", "scalar", "vector", "gpsimd", "sync"
ENGINES = [PE, ACT, DVE, POOL, SP]

D = 2048
NCH = 16
NHEAD = 8
DFF = 8192
EPS = 1e-6
T = 1024
TW = 512
NT = T // TW
SCALE = 128 ** -0.5
LAM0 = 0.8 - 0.6 * math.exp(-0.3 * 0)
NV = 232


class Op:
    __slots__ = ("eng", "fn", "reads", "writes", "is_dma", "deps", "signal",
                 "token", "idx", "dma_sem", "pre_waits", "is_cc")

    def __init__(self, eng, fn, reads, writes, is_dma):
        self.eng = eng
        self.fn = fn
        self.reads = reads
        self.writes = writes
        self.is_dma = is_dma
        self.deps = []
        self.signal = False
        self.token = None
        self.dma_sem = None
        self.pre_waits = []
        self.is_cc = False


class Rec:
    def __init__(self):
        self.calls = []

    def __getattr__(self, name):
        def m(*a, **k):
            self.calls.append((name, a, k))
            return self
        return m


def _replay(calls, eng):
    last = None
    for name, a, k in calls:
        last = getattr(eng, name)(*a, **k)
    return last


class Prog:
    def __init__(self, nc, n_dma_sems=32):
        self.nc = nc
        self.ops = []
        self.last_writer = {}
        self.readers = {}
        self.n_dma_sems = n_dma_sems

    def op(self, eng, fn, reads=(), writes=(), dma=False):
        rec = Rec()
        fn(rec)
        calls = rec.calls
        assert calls, "empty op"
        o = Op(eng, (lambda e, calls=calls: _replay(calls, e)), tuple(reads), tuple(writes), dma)
        o.idx = len(self.ops)
        deps = set()
        for k in o.reads:
            w = self.last_writer.get(k)
            if w is not None:
                deps.add(w)
        for k in o.writes:
            w = self.last_writer.get(k)
            if w is not None:
                deps.add(w)
            for r in self.readers.get(k, ()):
                deps.add(r)
        deps.discard(o.idx)
        o.deps = sorted(deps)
        for k in o.writes:
            self.last_writer[k] = o.idx
            self.readers[k] = []
        for k in o.reads:
            if k not in o.writes:
                self.readers.setdefault(k, []).append(o.idx)
        self.ops.append(o)
        return o

    def dma(self, eng, fn, reads=(), writes=()):
        return self.op(eng, fn, reads, writes, dma=True)

    def cc(self, eng, fn, reads=(), writes=()):
        o = self.op(eng, fn, reads, writes, dma=True)
        o.is_cc = True
        return o

    def emit(self, final_wait_keys=()):
        nc = self.nc
        ops = self.ops
        final_ops = set()
        for k in final_wait_keys:
            w = self.last_writer.get(k)
            if w is not None:
                final_ops.add(w)
        for o in ops:
            for d in o.deps:
                p = ops[d]
                if p.is_dma:
                    continue
                if p.eng == PE and o.eng == PE and not o.is_dma:
                    continue
                p.signal = True
        for d in final_ops:
            if not ops[d].is_dma:
                ops[d].signal = True
        eng_sem = {e: nc.alloc_semaphore("S_" + e) for e in ENGINES}
        dma_sems = [nc.alloc_semaphore("D_%d" % i) for i in range(self.n_dma_sems)]
        eng_cnt = {e: 0 for e in ENGINES}
        dma_cnt = [0] * self.n_dma_sems
        dma_last = [None] * self.n_dma_sems
        dma_rr = 0
        dma_rr_sw = 0
        cc_sem = None
        cc_cnt = 0
        for o in ops:
            if o.is_cc:
                cc_sem = nc.alloc_semaphore("CC%d" % cc_cnt)
                o.dma_sem = cc_sem
                o.token = (("C", cc_cnt), cc_sem, 1)
                cc_cnt += 1
            elif o.is_dma:
                half = self.n_dma_sems // 2
                if o.eng == POOL:
                    k = half + dma_rr_sw
                    dma_rr_sw = (dma_rr_sw + 1) % half
                else:
                    k = dma_rr
                    dma_rr = (dma_rr + 1) % half
                if dma_last[k] is not None:
                    o.pre_waits.append(ops[dma_last[k]].token)
                dma_cnt[k] += 16
                dma_last[k] = o.idx
                o.dma_sem = dma_sems[k]
                o.token = (("D", k), dma_sems[k], dma_cnt[k])
            elif o.signal:
                eng_cnt[o.eng] += 1
                o.token = (("E", o.eng), eng_sem[o.eng], eng_cnt[o.eng])
        waited = {e: {} for e in ENGINES}
        per_eng = {e: [] for e in ENGINES}
        for o in ops:
            m = {}
            toks = [ops[d].token for d in o.deps] + list(o.pre_waits)
            for t in toks:
                if t is None:
                    continue
                key, sem, val = t
                if key == ("E", PE) and o.eng == PE and not o.is_dma:
                    continue
                if waited[o.eng].get(key, 0) >= val:
                    continue
                waited[o.eng][key] = val
                if key not in m or m[key][1] < val:
                    m[key] = (sem, val)
            per_eng[o.eng].append((o, list(m.values())))
        final_tokens = [ops[d].token for d in sorted(final_ops)]

        def run_engine(ename, eng):
            for o, waits in per_eng[ename]:
                for sem, val in waits:
                    eng.wait_ge(sem, val)
                last = o.fn(eng)
                if o.is_cc:
                    last.then_inc(o.dma_sem, 1)
                elif o.is_dma:
                    last.then_inc(o.dma_sem, 16)
                elif o.signal:
                    last.then_inc(eng_sem[ename], 1)
            if ename == SP:
                for t in final_tokens:
                    eng.wait_ge(t[1], t[2])

        with nc.Block() as block:
            @block.tensor
            def _(e):
                run_engine(PE, e)

            @block.scalar
            def _(e):
                run_engine(ACT, e)

            @block.vector
            def _(e):
                run_engine(DVE, e)

            @block.gpsimd
            def _(e):
                run_engine(POOL, e)

            @block.sync
            def _(e):
                run_engine(SP, e)
        self.stats = dict(n_ops=len(ops), eng_cnt=dict(eng_cnt),
                          per_eng={e: len(v) for e, v in per_eng.items()})


def build(S, n_layers=2, S_ctx=0, n_cores=4):
    NPASS = S // T
    NCTX = S_ctx // T
    SK = S + S_ctx
    EXCH = S_ctx > 0
    nc = bass.Bass("TRN2", target_bir_lowering=False)

    def din(name, shape, dt=F32):
        return nc.dram_tensor(name, list(shape), dt, kind="ExternalInput").ap()

    xT = din("xT", [D, S])
    onescol_d = din("onescol", [128, 32])
    pT = din("pT", [2, 256, S])
    vecs_d = din("vecs", [128, NV])
    lam_d = din("lamb", [128, 512])
    alibi_d = din("alibi", [128, 256])
    tri_d = din("tri", [128, 128])
    w_qkv = din("w_qkv", [D, 6144])
    w_oa = din("w_o_attn", [D, D])
    w_in = din("w_in_rec", [D, 4096])
    w_ga = din("w_gate_a", [8, 256, 256])
    w_gx = din("w_gate_x", [8, 256, 256])
    w_or = din("w_o_rec", [D, D])
    w_up = din("w_up", [2, D, DFF])
    w_down = din("w_down", [2, DFF, D])
    w_pp = din("w_ple_proj", [2, 256, D])
    w_pg = din("w_ple_gate", [2, D, D])
    outT = nc.dram_tensor("outT", [D, S], F32, kind="ExternalOutput").ap()
    assert EXCH and S_ctx == S, "sequence-split mode only"
    alibic_d = din("alibic", [128, 256])
    Kc = [[nc.dram_tensor("Kc_%d_%d" % (p_, hh), [1024, T], BF16).ap() for hh in range(2)] for p_ in range(NPASS)]
    Vc = [[nc.dram_tensor("Vc_%d_%d" % (p_, hh), [T // 2, D], BF16).ap() for hh in range(2)] for p_ in range(NPASS)]
    KGc = [[nc.dram_tensor("KG_%d_%d" % (p_, hh), [2 * 1024, T], BF16).ap() for hh in range(2)] for p_ in range(NPASS)]
    VGc = [[nc.dram_tensor("VG_%d_%d" % (p_, hh), [2 * (T // 2), D], BF16).ap() for hh in range(2)] for p_ in range(NPASS)]
    Qscr = nc.dram_tensor("Qscr", [16 * 128, S], BF16).ap()
    if EXCH:
        Hscr = nc.dram_tensor("Hscr", [NPASS, D, T], F32).ap()
        cbounce = nc.dram_tensor("cbounce", [128, 80], F32).ap()
        cgath = nc.dram_tensor("cgath", [256, 80], F32).ap()

    arena = nc.alloc_sbuf_tensor("arena", [128, 16384], F32)
    arena_b = arena.bitcast(BF16)
    h = arena[:].rearrange("p (c n) -> p c n", c=NCH)
    kT = arena_b[:, 0:8192].rearrange("p (c n) -> p c n", c=2)
    Vx = arena_b[:, 8192:8192 + 32 * 258].rearrange("p (j d) -> p j d", d=258)
    qT = arena_b[:, 16448:16448 + 2048].rearrange("p (c n) -> p c n", c=2)
    Pb = [arena_b[:, 18496 + i * 512:18496 + (i + 1) * 512].rearrange("p (c n) -> p c n", c=2)
          for i in range(3)]
    onb = [arena_b[:, 20032 + i * 256:20032 + (i + 1) * 256] for i in range(2)]
    ot = [arena[:, 10272 + i * 256:10272 + (i + 1) * 256] for i in range(4)]

    xn = nc.alloc_sbuf_tensor("xn", [128, NCH, T], BF16)
    act2 = nc.alloc_sbuf_tensor("act2", [128, NCH, T], BF16)
    act2f = act2[:].rearrange("p c n -> p (c n)")
    slabs = [nc.alloc_sbuf_tensor("slab%d" % i, [128, 8192], BF16) for i in range(3)]
    sq_all = nc.alloc_sbuf_tensor("sq_all", [128, 4 * TW], BF16)
    sq = [sq_all[:, i * TW:(i + 1) * TW] for i in range(4)]
    sq_f = sq_all.bitcast(F32)
    sqf = [sq_f[:, i * TW:(i + 1) * TW] for i in range(2)]
    rstd = [nc.alloc_sbuf_tensor("rstd%d" % i, [128, TW], F32) for i in range(2)]
    ybuf = [nc.alloc_sbuf_tensor("ybuf%d" % i, [128, TW], BF16) for i in range(2)]
    xrb = [nc.alloc_sbuf_tensor("xrb%d" % i, [128, TW + 8], F32) for i in range(2)]
    xcf = [nc.alloc_sbuf_tensor("xcf%d" % i, [128, TW], F32) for i in range(2)]
    xcb = [nc.alloc_sbuf_tensor("xcb%d" % i, [128, TW], BF16) for i in range(2)]
    rbuf = nc.alloc_sbuf_tensor("rbuf", [128, TW], F32)
    igbuf = nc.alloc_sbuf_tensor("igbuf", [128, TW], F32)
    sm = nc.alloc_sbuf_tensor("sm", [128, 32], F32)
    onescol = nc.alloc_sbuf_tensor("onescol_sb", [128, 32], F32)
    carry = nc.alloc_sbuf_tensor("carry", [128, 80], F32)
    carry_in = nc.alloc_sbuf_tensor("carry_in", [128, 80], F32)
    hist = nc.alloc_sbuf_tensor("hist", [128, NCH, 4], F32)
    state = nc.alloc_sbuf_tensor("state", [128, NCH], F32)
    c8 = nc.alloc_sbuf_tensor("c8", [128, NCH], F32)
    c8h = nc.alloc_sbuf_tensor("c8h", [128, NCH], F32)
    eps_t = nc.alloc_sbuf_tensor("eps_t", [128, 2], F32)
    gsc = nc.alloc_sbuf_tensor("gsc", [128, 2], F32)
    hb = nc.alloc_sbuf_tensor("hb", [128, 2 * NCH], F32)
    neglam = nc.alloc_sbuf_tensor("neglam", [128, 8], F32)
    vecs = nc.alloc_sbuf_tensor("vecs_sb", [128, NV], F32)
    alibi = nc.alloc_sbuf_tensor("alibi_sb", [128, 256], F32)
    alibic = nc.alloc_sbuf_tensor("alibic_sb", [128, 256], F32)
    lam_sb = nc.alloc_sbuf_tensor("lam_sb", [128, 512], F32)
    tri = nc.alloc_sbuf_tensor("tri_sb", [128, 128], BF16)
    ident = nc.alloc_sbuf_tensor("ident", [128, 128], BF16)
    ones = nc.alloc_sbuf_tensor("ones", [128, 128], BF16)

    ps = [nc.alloc_psum_tensor("ps%d" % i, [128, 512], F32) for i in range(7)]
    pst = nc.alloc_psum_tensor("pst", [128, 1024], BF16)

    P = Prog(nc)

    G_MIX = [0, 48]
    G_MLP = [16, 64]
    G_PLE = [32, 80]
    CONVW = 96
    CONVB = 160
    BGA = 176
    BGX = 192
    LAMR = 208
    GQ = 224
    GK = 225

    st = dict(gp=0, slab=0, sq=0, rstd=0, stage=0, dq=0)
    slab_gen = [0, 0, 0]

    def gp_bank():
        b = st["gp"]
        st["gp"] = (b + 1) % 7
        return b

    def next_sq():
        i = st["sq"]
        st["sq"] = (i + 1) % 4
        return i

    def next_rstd():
        i = st["rstd"]
        st["rstd"] = (i + 1) % 2
        return i

    def next_stage():
        i = st["stage"]
        st["stage"] = (i + 1) % 16
        return i

    def hk(c, tt):
        return ("h", c, tt)

    def xk(c, tt):
        return ("xn", c, tt)

    def ak(c, tt):
        return ("act2", c, tt)

    def tok(tt):
        return slice(tt * TW, (tt + 1) * TW)

    def slab_load(parts):
        s = st["slab"]
        st["slab"] = (s + 1) % 3
        slab_gen[s] += 1
        keys = []
        for i, (dv, src) in enumerate(parts):
            k = ("slab", s, i)
            keys.append(k)
            dst = dv(slabs[s])
            P.dma(POOL, lambda e, dst=dst, src=src: e.dma_start(out=dst, in_=src), writes=[k])
        return dict(s=s, gen=slab_gen[s], keys=keys, t=slabs[s])

    def use(hd):
        assert slab_gen[hd["s"]] == hd["gen"], "slab slot reclaimed while live"
        return hd["keys"]

    def col_slab(W, c0, w, nk=16):
        src = W.rearrange("(kc p) n -> p kc n", p=128)
        half = nk // 2
        parts = []
        for i in range(2):
            k0, k1 = i * half, (i + 1) * half
            parts.append((lambda t, k0=k0, k1=k1: t[:, 0:nk * w].rearrange("p (k n) -> p k n", k=nk)[:, k0:k1, :],
                          src[:, k0:k1, c0:c0 + w]))
        hd = slab_load(parts)
        hd["v"] = hd["t"][:, 0:nk * w].rearrange("p (k n) -> p k n", k=nk)
        return hd

    def row_slab(W, r0, nk, ncols):
        src = W[r0:r0 + nk * 128, :].rearrange("(kc p) n -> p kc n", p=128)
        half = max(1, nk // 2)
        parts = []
        for i in range(nk // half):
            k0, k1 = i * half, (i + 1) * half
            parts.append((lambda t, k0=k0, k1=k1: t[:, 0:nk * ncols].rearrange("p (k n) -> p k n", k=nk)[:, k0:k1, :],
                          src[:, k0:k1, :]))
        hd = slab_load(parts)
        hd["v"] = hd["t"][:, 0:nk * ncols].rearrange("p (k n) -> p k n", k=nk)
        return hd

    def mm_group(out_ap, bank, lhs_fn, rhs_fn, nk, reads):
        def f(e):
            last = None
            for kc in range(nk):
                last = e.matmul(out_ap, lhsT=lhs_fn(kc), rhs=rhs_fn(kc), start=(kc == 0), stop=(kc == nk - 1))
            return last
        P.op(PE, f, reads=reads, writes=[("ps", bank)])

    P.dma(SP, lambda e: e.dma_start(out=vecs[:], in_=vecs_d), writes=["vecs"])
    P.dma(SP, lambda e: e.dma_start(out=lam_sb[:], in_=lam_d), writes=["lam_sb"])
    P.dma(SP, lambda e: e.dma_start(out=alibi[:], in_=alibi_d), writes=["alibi"])
    P.dma(SP, lambda e: e.dma_start(out=alibic[:], in_=alibic_d), writes=["alibic"])
    P.dma(POOL, lambda e: e.dma_start(out=tri[:], in_=tri_d), writes=["tri"])
    P.dma(SP, lambda e: e.dma_start(out=onescol[:], in_=onescol_d), writes=["onescol"])

    P.op(POOL, lambda e: e.memset(ident[:], 1.0), writes=["ident"])
    P.op(POOL, lambda e: e.affine_select(out=ident[:], in_=ident[:], pattern=[[-1, 128]], compare_op=ALU.is_equal,
                                         fill=0.0, base=0, channel_multiplier=1), reads=["ident"], writes=["ident"])
    P.op(DVE, lambda e: e.memset(ones[:], 1.0), writes=["ones"])
    P.op(DVE, lambda e: e.memset(eps_t[:], float(EPS)), writes=["eps"])
    P.op(DVE, lambda e: e.tensor_scalar_mul(out=gsc[:], in0=vecs[:, 226:228], scalar1=float(1.0 - LAM0)),
         reads=["vecs"], writes=["gsc"])
    P.op(DVE, lambda e: e.memset(hist[:], 0.0), writes=[("hist", cc) for cc in range(NCH)])
    P.op(DVE, lambda e: e.memset(state[:], 0.0), writes=[("state", cc) for cc in range(NCH)])
    P.op(DVE, lambda e: e.tensor_tensor(out=rbuf[:, 0:128], in0=lam_sb[:, 0:128], in1=lam_sb[:, 128:256], op=ALU.mult),
         reads=["lam_sb"], writes=["rbuf"])
    P.op(DVE, lambda e: e.tensor_tensor(out=igbuf[:, 0:128], in0=lam_sb[:, 256:384], in1=lam_sb[:, 384:512], op=ALU.mult),
         reads=["lam_sb"], writes=["igbuf"])
    P.op(DVE, lambda e: e.reduce_sum(out=neglam[:, 0:1], in_=rbuf[:, 0:128], axis=AX.X),
         reads=["rbuf"], writes=["nl0"])
    P.op(DVE, lambda e: e.reduce_sum(out=neglam[:, 1:2], in_=igbuf[:, 0:128], axis=AX.X),
         reads=["igbuf"], writes=["nl01"])
    P.op(ACT, lambda e: e.activation(out=neglam[:, 2:4], in_=neglam[:, 0:2], func=AF.Exp), reads=["nl0", "nl01"], writes=["nl23"])
    P.op(DVE, lambda e: e.tensor_tensor(out=neglam[:, 4:5], in0=neglam[:, 3:4], in1=neglam[:, 2:3], op=ALU.subtract),
         reads=["nl23"], writes=["nl4"])
    P.op(DVE, lambda e: e.tensor_scalar_add(out=neglam[:, 5:6], in0=neglam[:, 4:5], scalar1=float(-LAM0)),
         reads=["nl4"], writes=["neglam"])
    NEGLAM = neglam[:, 5:6]
    P.op(ACT, lambda e: e.activation(out=c8[:], in_=vecs[:, LAMR:LAMR + 16], func=AF.Exp, scale=-1.0),
         reads=["vecs"], writes=["c8a"])
    P.op(ACT, lambda e: e.activation(out=c8[:], in_=c8[:], func=AF.Ln, bias=1.0), reads=["c8a"], writes=["c8b"])
    P.op(DVE, lambda e: e.tensor_scalar_mul(out=c8[:], in0=c8[:], scalar1=-8.0), reads=["c8b"], writes=["c8"])
    P.op(DVE, lambda e: e.tensor_scalar_mul(out=c8h[:], in0=c8[:], scalar1=0.5), reads=["c8"], writes=["c8h"])
    P.op(DVE, lambda e: e.tensor_scalar_mul(out=hb[:], in0=vecs[:, BGA:BGA + 2 * NCH], scalar1=0.5), reads=["vecs"], writes=["hb"])

    KT_ALL = [("kT", c, w, pp) for c in range(2) for w in ("ctx", "own") for pp in range(NPASS)]
    VX_ALL = [("Vx", w, pp, vh) for w in ("ctx", "own") for pp in range(NPASS) for vh in range(2)]
    ARENA_ATT_KEYS = KT_ALL + VX_ALL + [("qT", 0, 0), ("qT", 0, 1), ("qT", 1, 0), ("qT", 1, 1)] + \
        [("Pb", i) for i in range(3)] + [("onb", i) for i in range(2)] + [("ot", i) for i in range(4)]
    H_KEYS = [hk(c, tt) for c in range(NCH) for tt in range(NT)]

    def rms_stat(src_ap_fn, src_keys_fn, nsrc, inv_n, sq_eng=None):
        bank = gp_bank()
        for c in range(nsrc):
            si = next_sq()
            if sq_eng is not None and sq_eng(c) == DVE:
                P.op(DVE, lambda e, c=c, si=si: e.tensor_tensor(out=sq[si][:], in0=src_ap_fn(c), in1=src_ap_fn(c), op=ALU.mult),
                     reads=src_keys_fn(c), writes=[("sq", si)])
            else:
                P.op(ACT, lambda e, c=c, si=si: e.activation(out=sq[si][:], in_=src_ap_fn(c), func=AF.Square),
                     reads=src_keys_fn(c), writes=[("sq", si)])
            P.op(PE, lambda e, c=c, si=si: e.matmul(ps[bank][:], lhsT=ones[:], rhs=sq[si][:],
                                                     start=(c == 0), stop=(c == nsrc - 1)),
                 reads=[("sq", si), "ones"], writes=[("ps", bank)])
        ri = next_rstd()
        P.op(ACT, lambda e: e.activation(out=rstd[ri][:], in_=ps[bank][:], func=AF.Sqrt, bias=eps_t[:, 0:1], scale=float(inv_n)),
             reads=[("ps", bank), "eps"], writes=[("rstd", ri)])
        P.op(DVE, lambda e: e.reciprocal(out=rstd[ri][:], in_=rstd[ri][:]),
             reads=[("rstd", ri)], writes=[("rstd", ri)])
        return ri

    def rmsnorm_h(gcol):
        for tt in range(NT):
            ri = rms_stat(lambda c: h[:, c, tok(tt)], lambda c: [hk(c, tt)], NCH, 1.0 / D,
                          sq_eng=(lambda c: DVE if c % 2 else ACT) if tt == 0 else None)
            for c in range(NCH):
                P.op(DVE, lambda e, c=c: e.scalar_tensor_tensor(out=xn[:, c, tok(tt)], in0=h[:, c, tok(tt)],
                                                                scalar=vecs[:, gcol + c:gcol + c + 1], in1=rstd[ri][:],
                                                                op0=ALU.mult, op1=ALU.mult),
                     reads=[hk(c, tt), ("rstd", ri), "vecs"], writes=[xk(c, tt)])

    def load_h(pos, src=None, reads=()):
        src = xT if src is None else src
        for i in range(4):
            P.dma(SP, lambda e, i=i: e.dma_start(out=h[:, 4 * i:4 * i + 4, :],
                                                 in_=src.rearrange("(c p) n -> p c n", p=128)[:, 4 * i:4 * i + 4, pos:pos + T]),
                  reads=list(reads), writes=[hk(c, tt) for c in range(4 * i, 4 * i + 4) for tt in range(NT)])

    def qk_norm_evac(bank, gcolumn, out_ap, out_keys, defer=None):
        si = next_sq()
        P.op(ACT, lambda e: e.activation(out=sq[si][:], in_=ps[bank][:], func=AF.Square),
             reads=[("ps", bank)], writes=[("sq", si)])
        if defer is not None:
            defer.append(lambda: _qk_norm_tail(bank, si, gcolumn, out_ap, out_keys))
        else:
            _qk_norm_tail(bank, si, gcolumn, out_ap, out_keys)

    def _qk_norm_tail(bank, si, gcolumn, out_ap, out_keys):
        b2 = gp_bank()
        P.op(PE, lambda e: e.matmul(ps[b2][:], lhsT=ones[:], rhs=sq[si][:], start=True, stop=True),
             reads=[("sq", si), "ones"], writes=[("ps", b2)])
        ri = next_rstd()
        P.op(ACT, lambda e: e.activation(out=rstd[ri][:], in_=ps[b2][:], func=AF.Sqrt, bias=eps_t[:, 0:1], scale=1.0 / 128),
             reads=[("ps", b2), "eps"], writes=[("rstd", ri)])
        P.op(DVE, lambda e: e.reciprocal(out=rstd[ri][:], in_=rstd[ri][:]),
             reads=[("rstd", ri)], writes=[("rstd", ri)])
        P.op(DVE, lambda e: e.scalar_tensor_tensor(out=out_ap, in0=ps[bank][:], scalar=vecs[:, gcolumn:gcolumn + 1],
                                                   in1=rstd[ri][:], op0=ALU.mult, op1=ALU.mult),
             reads=[("ps", bank), ("rstd", ri), "vecs"], writes=out_keys)

    def qkv_phase(p, rgp):
        pos = p * T
        xkeys = lambda tt: [xk(c, tt) for c in range(NCH)]

        def gather(which):
            for hh in range(2):
                if which == "K":
                    P.cc(POOL, lambda e, hh=hh: e.collective_compute("AllGather", ALU.bypass, replica_groups=rgp,
                                                                     ins=[Kc[p][hh]], outs=[KGc[p][hh]]),
                         reads=[("Kc", idx, p, tt) for idx in range(8 * hh, 8 * hh + 8) for tt in range(NT)],
                         writes=[("KG", p, hh)])
                else:
                    P.cc(POOL, lambda e, hh=hh: e.collective_compute("AllGather", ALU.bypass, replica_groups=rgp,
                                                                     ins=[Vc[p][hh]], outs=[VGc[p][hh]]),
                         reads=[("Vc", s_, p, tb) for s_ in range(4) for tb in range(4 * hh, 4 * hh + 4)],
                         writes=[("VG", p, hh)])

        def qk_part(c0, gcol, kname, after_first=None):
            tails = []
            for s_ in range(4):
                sl = col_slab(w_qkv, c0 + 512 * s_, 512)
                for m in range(4):
                    idx = 4 * s_ + m
                    for tt in range(NT):
                        bank = gp_bank()
                        mm_group(ps[bank][:], bank, lambda kc: sl["v"][:, kc, m * 128:(m + 1) * 128],
                                 lambda kc: xn[:, kc, tok(tt)], NCH, use(sl) + xkeys(tt))
                        sg = next_stage()
                        dst = act2[:, sg, 0:TW]
                        prev_tail = list(tails)
                        del tails[:]
                        qk_norm_evac(bank, gcol, dst, [ak(sg, 0)], defer=tails)
                        if kname == "Kc":
                            r0 = (idx % 8) * 128
                            out_ap = Kc[p][idx // 8][r0:r0 + 128, tt * TW:(tt + 1) * TW]
                        else:
                            out_ap = Qscr[idx * 128:(idx + 1) * 128, pos + tt * TW:pos + (tt + 1) * TW]
                        tails.append(lambda dst=dst, out_ap=out_ap, sg=sg, idx=idx, tt=tt: P.dma(
                            SP, lambda e: e.dma_start(out=out_ap, in_=dst),
                            reads=[ak(sg, 0)], writes=[(kname, idx, p, tt)]))
                        for f_ in prev_tail:
                            f_()
                if s_ == 1 and after_first is not None:
                    after_first()
            for f_ in tails:
                f_()
            del tails[:]

        def v_part(after_first=None):
            for s_ in range(4):
                sl = col_slab(w_qkv, 4096 + 512 * s_, 512)
                for tb in range(T // 128):
                    bank = gp_bank()
                    mm_group(ps[bank][:], bank, lambda kc: xn[:, kc, tb * 128:(tb + 1) * 128],
                             lambda kc: sl["v"][:, kc, :], NCH, use(sl) + xkeys(tb // 4))
                    sg = next_stage()
                    dst = act2[:, sg, 0:TW]
                    P.op(ACT, lambda e, dst=dst, bank=bank: e.activation(out=dst, in_=ps[bank][:], func=AF.Copy),
                         reads=[("ps", bank)], writes=[ak(sg, 0)])
                    vr0 = (tb % 4) * 128
                    P.dma(SP, lambda e, dst=dst, tb=tb, s_=s_, vr0=vr0: e.dma_start(
                        out=Vc[p][tb // 4][vr0:vr0 + 128, s_ * 512:(s_ + 1) * 512], in_=dst),
                        reads=[ak(sg, 0)], writes=[("Vc", s_, p, tb)])
                if s_ == 1 and after_first is not None:
                    after_first()

        qk_part(2048, GK, "Kc")
        v_part(after_first=lambda: gather("K"))
        qk_part(0, GQ, "Qscr", after_first=lambda: gather("V"))

    def att_loads(p_own, hd):
        nown = (p_own + 1) * T
        ncb = S_ctx // 128
        for c in range(2):
            idx = 2 * hd + c
            kh, r0 = idx // 8, (idx % 8) * 128
            for pp in range(NCTX):
                P.dma(SP, lambda e, c=c, pp=pp: e.dma_start(out=kT[:, c, pp * T:(pp + 1) * T], in_=KGc[pp][kh][r0:r0 + 128, :]),
                      reads=[("KG", pp, kh)], writes=[("kT", c, "ctx", pp)])
            for pp in range(p_own + 1):
                P.dma(SP, lambda e, c=c, pp=pp: e.dma_start(out=kT[:, c, S_ctx + pp * T:S_ctx + (pp + 1) * T],
                                                            in_=Kc[pp][kh][r0:r0 + 128, :]),
                      reads=[("Kc", idx, pp, tt) for tt in range(NT)], writes=[("kT", c, "own", pp)])
            P.dma(SP, lambda e, c=c: e.dma_start(out=qT[:, c, :], in_=Qscr[idx * 128:(idx + 1) * 128, p_own * T:(p_own + 1) * T]),
                  reads=[("Qscr", idx, p_own, tt) for tt in range(NT)],
                  writes=[("qT", c, 0), ("qT", c, 1)])
        for pp in range(NCTX):
            for vh in range(2):
                b0 = pp * 8 + vh * 4
                P.dma(SP, lambda e, pp=pp, vh=vh, b0=b0: e.dma_start(
                    out=Vx[:, b0:b0 + 4, 0:256],
                    in_=VGc[pp][vh][0:T // 2, hd * 256:(hd + 1) * 256].rearrange("(j q) d -> q j d", q=128)),
                    reads=[("VG", pp, vh)], writes=[("Vx", "ctx", pp, vh)])
        for pp in range(p_own + 1):
            for vh in range(2):
                b0 = ncb + pp * 8 + vh * 4
                P.dma(SP, lambda e, pp=pp, vh=vh, b0=b0: e.dma_start(
                    out=Vx[:, b0:b0 + 4, 0:256],
                    in_=Vc[pp][vh][:, hd * 256:(hd + 1) * 256].rearrange("(j q) d -> q j d", q=128)),
                    reads=[("Vc", hd // 2, pp, tb) for tb in range(4 * vh, 4 * vh + 4)], writes=[("Vx", "own", pp, vh)])

    def att_prefetch(p_own):
        P.op(DVE, lambda e: e.tensor_copy(out=Vx[:, :, 256:257], in_=onescol[:, :].rearrange("p (j o) -> p j o", o=1)),
             reads=["onescol"], writes=H_KEYS + ARENA_ATT_KEYS)
        att_loads(p_own, 0)

    def attention(p_own, prefetched=False):
        p = p_own + NCTX
        pos = p * T
        nk = (p + 1) * T
        nown = (p_own + 1) * T
        ncb = S_ctx // 128
        if not prefetched:
            att_prefetch(p_own)
        for hd in range(NHEAD):
            if hd > 0:
                att_loads(p_own, hd)
            def kpiece(j):
                w, jj = ("ctx", j) if j < ncb else ("own", j - ncb)
                return [("kT", 0, w, jj // 8), ("kT", 1, w, jj // 8)]

            def vpiece(j):
                w, jj = ("ctx", j) if j < ncb else ("own", j - ncb)
                return [("Vx", w, jj // 8, (jj % 8) // 4)]
            acc = [[0, 1], [2, 3]]
            pending = []

            def flush_pending():
                while pending:
                    pending.pop(0)()

            def _part1(qb, hd, q0l, tt):
                a0, a1 = acc[qb]
                o0 = ot[qb]
                ob = onb[qb]
                sc = 8 * qb
                kk = lambda i, qb=qb: ("sm", qb, i)
                P.op(DVE, lambda e: e.reciprocal(out=sm[:, sc:sc + 1], in_=ps[a0][:, 256:257]),
                     reads=[("ps", a0)], writes=[kk(0)])
                P.op(DVE, lambda e: e.reciprocal(out=sm[:, sc + 1:sc + 2], in_=ps[a1][:, 256:257]),
                     reads=[("ps", a1)], writes=[kk(1)])
                P.op(DVE, lambda e: e.tensor_tensor(out=sm[:, sc + 2:sc + 3], in0=sm[:, sc + 1:sc + 2], in1=NEGLAM,
                                                    op=ALU.mult), reads=[kk(1), "neglam"], writes=[kk(2)])
                P.op(DVE, lambda e: e.tensor_scalar(out=o0, in0=ps[a0][:, 0:256], scalar1=sm[:, sc:sc + 1], scalar2=None,
                                                    op0=ALU.mult), reads=[("ps", a0), kk(0)], writes=[("ot", qb)])
                P.op(DVE, lambda e: e.scalar_tensor_tensor(out=ob, in0=ps[a1][:, 0:256], scalar=sm[:, sc + 2:sc + 3], in1=o0,
                                                           op0=ALU.mult, op1=ALU.add),
                     reads=[("ps", a1), kk(2), ("ot", qb)], writes=[("onb", qb)])

            for t in range(T // 256):
                Q0 = pos + 256 * t
                q0l = 256 * t
                jb = Q0 // 128
                jlast = jb + 1
                tt = q0l // TW

                def qk(j):
                    m = jb - j
                    qa = 128 if m == -1 else 0
                    sb = 4 + (j % 3)
                    sv = ps[sb][:].rearrange("p (c n) -> p c n", c=2)

                    def f(e):
                        last = None
                        for c in range(2):
                            last = e.matmul(sv[:, c, qa:256], lhsT=kT[:, c, j * 128:(j + 1) * 128],
                                            rhs=qT[:, c, q0l + qa:q0l + 256], start=True, stop=True)
                        return last
                    P.op(PE, f, reads=kpiece(j) + [("qT", 0, tt), ("qT", 1, tt)], writes=[("ps", sb)])
                    pi = j % 3
                    col = hd * 32 + (m + 1)
                    atab = alibic if j < ncb else alibi
                    P.op(ACT, lambda e: e.activation(out=Pb[pi][:, :, qa:256], in_=sv[:, :, qa:256], func=AF.Exp,
                                                     bias=atab[:, col:col + 1], scale=float(SCALE)),
                         reads=[("ps", sb), "alibi", "alibic"], writes=[("Pb", pi)])
                    if m == 0 or m == -1:
                        lo = 0 if m == 0 else 128

                        def fm(e):
                            last = None
                            for c in range(2):
                                last = e.tensor_tensor(out=Pb[pi][:, c, lo:lo + 128], in0=Pb[pi][:, c, lo:lo + 128],
                                                       in1=tri[:], op=ALU.mult)
                            return last
                        P.op(DVE, fm, reads=[("Pb", pi), "tri"], writes=[("Pb", pi)])

                def pv(j):
                    m = jb - j
                    pi = j % 3
                    for qb in range(2):
                        if m == -1 and qb == 0:
                            continue
                        lastj = jb if qb == 0 else jb + 1

                        def f(e, qb=qb, lastj=lastj):
                            last = None
                            for c in range(2):
                                last = e.matmul(ps[acc[qb][c]][:, 0:257], lhsT=Pb[pi][:, c, qb * 128:(qb + 1) * 128],
                                                rhs=Vx[:, j, 0:257], start=(j == 0), stop=(j == lastj))
                            return last
                        P.op(PE, f, reads=[("Pb", pi)] + vpiece(j), writes=[("ps", acc[qb][0]), ("ps", acc[qb][1])])

                qk(0)
                qk(1)
                def part1(qb):
                    return _part1(qb, hd, q0l, tt)

                for j in range(jlast + 1):
                    if j + 2 <= jlast:
                        qk(j + 2)
                    pv(j)
                    if j == 2:
                        flush_pending()
                    if j == jb:
                        part1(0)
                flush_pending()
                part1(1)
                def part2(qb, q0l=q0l, tt=tt):
                    ob = onb[qb]
                    tsl = slice(q0l + qb * 128, q0l + (qb + 1) * 128)

                    def ftr(e):
                        last = None
                        for half in range(2):
                            last = e.transpose(pst[:, half * 128:(half + 1) * 128], ob[:, half * 128:(half + 1) * 128], ident[:])
                        return last
                    P.op(PE, ftr, reads=[("onb", qb), "ident"], writes=[("pst",)])
                    P.op(DVE, lambda e: e.tensor_copy(out=act2[:, 2 * hd:2 * hd + 2, tsl],
                                                      in_=pst[:, 0:256].rearrange("p (a b) -> p a b", a=2)),
                         reads=[("pst",)], writes=[ak(2 * hd, tt), ak(2 * hd + 1, tt)])
                pending.append(lambda: part2(0))
                pending.append(lambda: part2(1))
            flush_pending()
            for tt in range(NT):
                ri = rms_stat(lambda c: act2[:, 2 * hd + c, tok(tt)], lambda c: [ak(2 * hd + c, tt)], 2, 1.0 / 256)
                for c in range(2):
                    P.op(DVE, lambda e, c=c: e.scalar_tensor_tensor(out=act2[:, 2 * hd + c, tok(tt)], in0=act2[:, 2 * hd + c, tok(tt)],
                                                                    scalar=gsc[:, c:c + 1], in1=rstd[ri][:],
                                                                    op0=ALU.mult, op1=ALU.mult),
                         reads=[ak(2 * hd + c, tt), ("rstd", ri), "gsc"], writes=[ak(2 * hd + c, tt)])
        P.op(DVE, lambda e: e.memset(sm[:, 31:32], 0.0), reads=[], writes=H_KEYS + ARENA_ATT_KEYS)

    def proj_add_h(W, in_buf, in_key):
        for s in range(4):
            sl = col_slab(W, 512 * s, 512)
            order = [(m, tt) for m in range(4) for tt in range(NT)] if s < 3 else [(m, tt) for tt in range(NT) for m in range(4)]
            for (m, tt) in order:
                oc = 4 * s + m
                if True:
                    bank = gp_bank()
                    mm_group(ps[bank][:], bank, lambda kc: sl["v"][:, kc, m * 128:(m + 1) * 128],
                             lambda kc: in_buf[:, kc, tok(tt)], NCH, use(sl) + [in_key(kc, tt) for kc in range(NCH)])
                    P.op(DVE, lambda e, oc=oc, tt=tt, bank=bank: e.tensor_tensor(out=h[:, oc, tok(tt)], in0=h[:, oc, tok(tt)],
                                                                                 in1=ps[bank][:], op=ALU.add),
                         reads=[("ps", bank), hk(oc, tt)], writes=[hk(oc, tt)])

    def mlp(l):
        rmsnorm_h(G_MLP[l])
        NS = DFF // 512
        wu = [None] * NS
        wd = [None] * NS

        def up(s):
            sl = wu[s]
            hb = s % 2
            for m in range(4):
                for tt in range(NT):
                    bank = gp_bank()
                    mm_group(ps[bank][:], bank, lambda kc: sl["v"][:, kc, m * 128:(m + 1) * 128],
                             lambda kc: xn[:, kc, tok(tt)], NCH, use(sl) + [xk(kc, tt) for kc in range(NCH)])
                    dst = act2[:, 4 * hb + m, tok(tt)]
                    P.op(ACT, lambda e, dst=dst, bank=bank: e.activation(out=dst, in_=ps[bank][:], func=AF.Relu),
                         reads=[("ps", bank)], writes=[ak(4 * hb + m, tt)])
                    P.op(ACT, lambda e, dst=dst: e.activation(out=dst, in_=dst, func=AF.Square),
                         reads=[ak(4 * hb + m, tt)], writes=[ak(4 * hb + m, tt)])

        def down(s):
            sl = wd[s]
            hb = s % 2
            order = [(oc, tt) for oc in range(NCH) for tt in range(NT)] if s < NS - 1 else \
                [(oc, tt) for tt in range(NT) for oc in range(NCH)]
            for (oc, tt) in order:
                if True:
                    bank = gp_bank()
                    mm_group(ps[bank][:], bank, lambda kc: sl["v"][:, kc, oc * 128:(oc + 1) * 128],
                             lambda kc: act2[:, 4 * hb + kc, tok(tt)], 4, use(sl) + [ak(4 * hb + kc, tt) for kc in range(4)])
                    P.op(DVE, lambda e, oc=oc, tt=tt, bank=bank: e.tensor_tensor(out=h[:, oc, tok(tt)], in0=h[:, oc, tok(tt)],
                                                                                 in1=ps[bank][:], op=ALU.add),
                         reads=[("ps", bank), hk(oc, tt)], writes=[hk(oc, tt)])

        wu[0] = col_slab(w_up[l], 0, 512)
        up(0)
        for s in range(1, NS):
            wu[s] = col_slab(w_up[l], 512 * s, 512)
            wd[s - 1] = row_slab(w_down[l], 512 * (s - 1), 4, D)
            up(s)
            down(s - 1)
        wd[NS - 1] = row_slab(w_down[l], 512 * (NS - 1), 4, D)
        down(NS - 1)

    def ple(l, p):
        pos = p * T
        rmsnorm_h(G_PLE[l])
        pb = act2[:, 0:2, :]
        P.dma(POOL, lambda e: e.dma_start(out=pb, in_=pT[l].rearrange("(kc q) n -> q kc n", q=128)[:, :, pos:pos + T]),
              writes=[ak(c, tt) for c in range(2) for tt in range(NT)])
        wpp = act2f[:, 2048:2048 + 4096].rearrange("p (k n) -> p k n", k=2)
        P.dma(POOL, lambda e: e.dma_start(out=wpp, in_=w_pp[l].rearrange("(kc q) n -> q kc n", q=128)),
              writes=[ak(c, tt) for c in range(2, 6) for tt in range(NT)])
        wpp_keys = [ak(c, tt) for c in range(2, 6) for tt in range(NT)]
        tb = [rbuf, igbuf]
        tbk = ["rbuf", "igbuf"]
        n = 0
        for s in range(4):
            sl = col_slab(w_pg[l], 512 * s, 512)
            order = [(m, tt) for m in range(4) for tt in range(NT)] if s < 3 else [(m, tt) for tt in range(NT) for m in range(4)]
            for (m, tt) in order:
                oc = 4 * s + m
                if True:
                    bg = gp_bank()
                    mm_group(ps[bg][:], bg, lambda kc: sl["v"][:, kc, m * 128:(m + 1) * 128],
                             lambda kc: xn[:, kc, tok(tt)], NCH, use(sl) + [xk(kc, tt) for kc in range(NCH)])
                    bp = gp_bank()
                    mm_group(ps[bp][:], bp, lambda kc: wpp[:, kc, oc * 128:(oc + 1) * 128],
                             lambda kc: pb[:, kc, tok(tt)], 2, wpp_keys + [ak(0, tt), ak(1, tt)])
                    tbi = n % 2
                    n += 1
                    tbuf, tkey = tb[tbi], tbk[tbi]
                    P.op(ACT, lambda e, tbuf=tbuf, bg=bg: e.activation(out=tbuf[:], in_=ps[bg][:], func=AF.Sigmoid),
                         reads=[("ps", bg)], writes=[tkey])
                    P.op(DVE, lambda e, tbuf=tbuf, bp=bp: e.tensor_tensor(out=tbuf[:], in0=tbuf[:], in1=ps[bp][:], op=ALU.mult),
                         reads=[tkey, ("ps", bp)], writes=[tkey])
                    P.op(DVE, lambda e, tbuf=tbuf, oc=oc, tt=tt: e.tensor_tensor(out=h[:, oc, tok(tt)], in0=h[:, oc, tok(tt)],
                                                                                 in1=tbuf[:], op=ALU.add),
                         reads=[tkey, hk(oc, tt)], writes=[hk(oc, tt)])

    def rglru_block_slabs(n, lite):
        srcw = w_in.rearrange("(kc q) n -> q kc n", q=128)
        partsA = [
            (lambda t: t[:, 0:8192].rearrange("p (k n) -> p k n", k=16)[:, :, 0:256], srcw[:, :, n * 256:(n + 1) * 256]),
            (lambda t: t[:, 0:8192].rearrange("p (k n) -> p k n", k=16)[:, :, 256:512],
             srcw[:, :, 2048 + n * 256:2048 + (n + 1) * 256]),
        ]
        if lite:
            partsA = partsA[1:]
        sA = slab_load(partsA)
        sA["v"] = sA["t"][:, 0:8192].rearrange("p (k n) -> p k n", k=16)
        partsG = [
            (lambda t: t[:, 0:1024].rearrange("p (k n) -> p k n", k=2)[:, :, 0:256],
             w_ga[n].rearrange("(kc q) n -> q kc n", q=128)),
            (lambda t: t[:, 0:1024].rearrange("p (k n) -> p k n", k=2)[:, :, 256:512],
             w_gx[n].rearrange("(kc q) n -> q kc n", q=128)),
        ]
        sG = slab_load(partsG)
        sG["v"] = sG["t"][:, 0:1024].rearrange("p (k n) -> p k n", k=2)
        return sA, sG

    def rglru(lite=False, pre=None):
        rb = [rbuf, rstd[0]]
        rbk = ["rbuf", ("rstd", 0)]
        ib = [igbuf, rstd[1]]
        ibk = ["igbuf", ("rstd", 1)]
        XC = [[xcf[0], xcf[1]], [sqf[0], sqf[1]]]
        XCK = [[[("xcf", 0)], [("xcf", 1)]], [[("sq", 0), ("sq", 1)], [("sq", 2), ("sq", 3)]]]
        XB = [[xcb[0], xcb[1]], [ybuf[0], ybuf[1]]]
        XBK = [[("xcb", 0), ("xcb", 1)], [("ybuf", 0), ("ybuf", 1)]]
        slabsG = {}

        def stageA(i, n, tt, sA):
            par = i % 2
            xkeys = [xk(kc, tt) for kc in range(NCH)]
            for j in range(2):
                cc = 2 * n + j
                xm, xh = ("xrb_main", j), ("xrb_hist", j)
                xc, xck = XC[par][j], XCK[par][j]
                if not lite:
                    bank = gp_bank()
                    mm_group(ps[bank][:], bank, lambda kc: sA["v"][:, kc, j * 128:(j + 1) * 128],
                             lambda kc: xn[:, kc, tok(tt)], NCH, use(sA) + xkeys)
                    P.op(ACT, lambda e: e.activation(out=act2[:, cc, tok(tt)], in_=ps[bank][:], func=AF.Gelu_apprx_tanh),
                         reads=[("ps", bank)], writes=[ak(cc, tt)])
                bank2 = gp_bank()
                mm_group(ps[bank2][:], bank2, lambda kc: sA["v"][:, kc, 256 + j * 128:256 + (j + 1) * 128],
                         lambda kc: xn[:, kc, tok(tt)], NCH, use(sA) + xkeys)
                P.op(ACT, lambda e: e.activation(out=xrb[j][:, 3:3 + TW], in_=ps[bank2][:], func=AF.Copy),
                     reads=[("ps", bank2)], writes=[xm])
                P.op(DVE, lambda e: e.tensor_copy(out=xrb[j][:, 0:3], in_=hist[:, cc, 0:3]),
                     reads=[("hist", cc)], writes=[xh])
                cw = lambda jj, cc=cc: vecs[:, CONVW + 16 * jj + cc:CONVW + 16 * jj + cc + 1]
                P.op(ACT, lambda e: e.activation(out=xc[:], in_=ps[bank2][:], func=AF.Identity,
                                                 bias=vecs[:, CONVB + cc:CONVB + cc + 1], scale=cw(3)),
                     reads=[("ps", bank2), "vecs"], writes=xck)
                for jj in (2, 1, 0):
                    P.op(DVE, lambda e, jj=jj: e.scalar_tensor_tensor(out=xc[:], in0=xrb[j][:, jj:jj + TW], scalar=cw(jj),
                                                                      in1=xc[:], op0=ALU.mult, op1=ALU.add),
                         reads=[xm, xh, "vecs"] + xck, writes=xck)
                P.op(DVE, lambda e: e.tensor_copy(out=hist[:, cc, 0:3], in_=xrb[j][:, TW:TW + 3]),
                     reads=[xm], writes=[("hist", cc)])

        def stageA2(i):
            par = i % 2
            for j in range(2):
                P.op(ACT, lambda e: e.activation(out=XB[par][j][:], in_=XC[par][j][:], func=AF.Copy),
                     reads=XCK[par][j], writes=[XBK[par][j]])

        def stageB(i, n, tt, sG):
            par = i % 2
            brs, bis = [], []
            for j in range(2):
                br = gp_bank()
                mm_group(ps[br][:], br, lambda kc: sG["v"][:, kc, j * 128:(j + 1) * 128],
                         lambda kc: XB[par][kc][:], 2, use(sG) + XBK[par])
                bi = gp_bank()
                mm_group(ps[bi][:], bi, lambda kc: sG["v"][:, kc, 256 + j * 128:256 + (j + 1) * 128],
                         lambda kc: XB[par][kc][:], 2, use(sG) + XBK[par])
                brs.append(br)
                bis.append(bi)
            for j in range(2):
                cc = 2 * n + j
                P.op(ACT, lambda e: e.activation(out=rb[j][:], in_=ps[brs[j]][:], func=AF.Tanh,
                                                 bias=hb[:, cc:cc + 1], scale=0.5),
                     reads=[("ps", brs[j]), "hb"], writes=[rbk[j]])
                P.op(ACT, lambda e: e.activation(out=ib[j][:], in_=ps[bis[j]][:], func=AF.Tanh,
                                                 bias=hb[:, NCH + cc:NCH + cc + 1], scale=0.5),
                     reads=[("ps", bis[j]), "hb"], writes=[ibk[j]])
            for j in range(2):
                cc = 2 * n + j
                xc, xck = XC[par][j], XCK[par][j]
                P.op(ACT, lambda e: e.activation(out=rb[j][:], in_=rb[j][:], func=AF.Exp,
                                                 bias=c8h[:, cc:cc + 1], scale=c8h[:, cc:cc + 1]),
                     reads=[rbk[j], "c8h"], writes=[rbk[j]])
                P.op(DVE, lambda e: e.scalar_tensor_tensor(out=ib[j][:], in0=ib[j][:], scalar=1.0, in1=xc[:],
                                                           op0=ALU.add, op1=ALU.mult),
                     reads=[ibk[j]] + xck, writes=[ibk[j]])
            for j in range(2):
                xc, xck = XC[par][j], XCK[par][j]
                P.op(ACT, lambda e: e.activation(out=xc[:], in_=rb[j][:], func=AF.Square),
                     reads=[rbk[j]], writes=xck)
            for j in range(2):
                xc, xck = XC[par][j], XCK[par][j]
                P.op(ACT, lambda e: e.activation(out=xc[:], in_=xc[:], func=AF.Sqrt, bias=0.25, scale=-0.25),
                     reads=xck, writes=xck)
            for j in range(2):
                cc = 2 * n + j
                xc, xck = XC[par][j], XCK[par][j]
                P.op(DVE, lambda e: e.tensor_tensor(out=ib[j][:], in0=ib[j][:], in1=xc[:], op=ALU.mult),
                     reads=[ibk[j]] + xck, writes=[ibk[j]])
                P.op(DVE, lambda e: e.tensor_tensor_scan(out=xc[:], data0=rb[j][:], data1=ib[j][:],
                                                         initial=state[:, cc:cc + 1], op0=ALU.mult, op1=ALU.add),
                     reads=[rbk[j], ibk[j], ("state", cc)] + xck, writes=xck)
                P.op(DVE, lambda e: e.tensor_copy(out=state[:, cc:cc + 1], in_=xc[:, TW - 1:TW]),
                     reads=xck, writes=[("state", cc)])
                if not lite:
                    P.op(DVE, lambda e: e.tensor_tensor(out=act2[:, cc, tok(tt)], in0=xc[:], in1=act2[:, cc, tok(tt)], op=ALU.mult),
                         reads=xck + [ak(cc, tt)], writes=[ak(cc, tt)])

        steps = [(n, tt) for n in range(8) for tt in range(NT)]
        prev = None
        sA = None
        for i, (n, tt) in enumerate(steps):
            if tt == 0:
                if n == 0 and pre is not None:
                    sA, sG = pre
                else:
                    sA, sG = rglru_block_slabs(n, lite)
                slabsG[n] = sG
            stageA(i, n, tt, sA)
            if prev is not None:
                stageB(prev[0], prev[1], prev[2], slabsG[prev[1]])
            stageA2(i)
            prev = (i, n, tt)
        stageB(prev[0], prev[1], prev[2], slabsG[prev[1]])

    out_keys = []

    def layer0(p_own):
        pos = p_own * T
        attention(p_own, prefetched=(p_own > 0))
        load_h(pos)
        proj_add_h(w_oa, act2, ak)
        mlp(0)
        ple(0, p_own)

    def layer1(p_own, norm_done=False, pre=None):
        if not norm_done:
            rmsnorm_h(G_MIX[1])
        rglru(pre=pre)
        proj_add_h(w_or, act2, ak)
        mlp(1)
        ple(1, p_own)

    def store_out(p_own):
        pos = p_own * T
        for i in range(4):
            k = ("out", p_own, i)
            out_keys.append(k)
            P.dma(SP, lambda e, i=i, pos=pos: e.dma_start(
                out=outT.rearrange("(c q) n -> q c n", q=128)[:, 4 * i:4 * i + 4, pos:pos + T], in_=h[:, 4 * i:4 * i + 4, :]),
                reads=[hk(c, tt) for c in range(4 * i, 4 * i + 4) for tt in range(NT)], writes=[k])

    if True:
        rgp = [[2 * i, 2 * i + 1] for i in range(n_cores // 2)]
        load_h(0)
        for p in range(NPASS):
            rmsnorm_h(G_MIX[0])
            if p + 1 < NPASS:
                load_h((p + 1) * T)
            qkv_phase(p, rgp)
        for p in range(NPASS):
            layer0(p)
            for i in range(4):
                P.dma(SP, lambda e, i=i, p=p: e.dma_start(
                    out=Hscr[p].rearrange("(c q) n -> q c n", q=128)[:, 4 * i:4 * i + 4, :], in_=h[:, 4 * i:4 * i + 4, :]),
                    reads=[hk(c, tt) for c in range(4 * i, 4 * i + 4) for tt in range(NT)], writes=[("Hscr", p, i)])
            rmsnorm_h(G_MIX[1])
            if p == NPASS - 1:
                load_h(0, src=Hscr[0], reads=[("Hscr", 0, i) for i in range(4)])
            else:
                att_prefetch(p + 1)
            rglru(lite=True)
        pre0 = rglru_block_slabs(0, False)
        P.op(DVE, lambda e: e.tensor_copy(out=carry[:, 0:16], in_=state[:]),
             reads=[("state", cc) for cc in range(NCH)], writes=["carry_a"])
        P.op(DVE, lambda e: e.tensor_copy(out=carry[:, 16:80], in_=hist[:].rearrange("p c k -> p (c k)")),
             reads=[("hist", cc) for cc in range(NCH)], writes=["carry_b"])
        P.dma(SP, lambda e: e.dma_start(out=cbounce, in_=carry[:]), reads=["carry_a", "carry_b"], writes=["cbounce"])
        rg = [[2 * i, 2 * i + 1] for i in range(n_cores // 2)]
        P.cc(POOL, lambda e: e.collective_compute("AllGather", ALU.bypass, replica_groups=rg, ins=[cbounce], outs=[cgath]),
             reads=["cbounce"], writes=["cgath"])
        rmsnorm_h(G_MIX[1])
        P.dma(SP, lambda e: e.dma_start(out=carry_in[:], in_=cgath[0:128, :]), reads=["cgath"], writes=["carry_in"])
        P.op(DVE, lambda e: e.tensor_scalar(out=carry_in[:], in0=carry_in[:], scalar1=vecs[:, 228:229], scalar2=None, op0=ALU.mult),
             reads=["carry_in", "vecs"], writes=["carry_in"])
        P.op(DVE, lambda e: e.tensor_copy(out=state[:], in_=carry_in[:, 0:16]),
             reads=["carry_in"], writes=[("state", cc) for cc in range(NCH)])
        P.op(DVE, lambda e: e.tensor_copy(out=hist[:].rearrange("p c k -> p (c k)"), in_=carry_in[:, 16:80]),
             reads=["carry_in"], writes=[("hist", cc) for cc in range(NCH)])
        for p in range(NPASS):
            if p > 0:
                load_h(0, src=Hscr[p], reads=[("Hscr", p, i) for i in range(4)])
            layer1(p, norm_done=(p == 0), pre=(pre0 if p == 0 else None))
            store_out(p)
    P.emit(final_wait_keys=out_keys)
    return nc, P


def _fv(v):
    return np.ascontiguousarray(np.asarray(v, np.float32).reshape(NCH, 128).T)


def _consts():
    kl = np.arange(128, dtype=np.float32)[:, None]
    alibi = np.zeros((128, 256), np.float32)
    for hh in range(NHEAD):
        slope = 2.0 ** (-8.0 * (hh + 1) / NHEAD)
        for mi in range(32):
            alibi[:, hh * 32 + mi] = (slope * (kl[:, 0] - 128.0 * mi)).astype(np.float32)
    tri = (np.arange(128)[:, None] <= np.arange(128)[None, :]).astype(np.float32)
    return alibi, tri


def make_in_maps(S, batches, x, p, g_mix, g_mlp, g_ple, w_qkv, g_q, g_k, lam_q1, lam_k1, lam_q2,
                 lam_k2, g_sub, w_o_attn, w_in_rec, conv_w, conv_b, w_gate_a, b_gate_a,
                 w_gate_x, b_gate_x, lam_rec, w_o_rec, w_up, w_down, w_ple_proj, w_ple_gate, split=False):
    f = lambda a: np.ascontiguousarray(np.asarray(a, np.float32))
    vecs = np.zeros((128, NV), np.float32)
    vecs[:, 0:16] = _fv(g_mix[0]); vecs[:, 16:32] = _fv(g_mlp[0]); vecs[:, 32:48] = _fv(g_ple[0])
    vecs[:, 48:64] = _fv(g_mix[1]); vecs[:, 64:80] = _fv(g_mlp[1]); vecs[:, 80:96] = _fv(g_ple[1])
    for j in range(4):
        vecs[:, 96 + 16 * j:96 + 16 * (j + 1)] = _fv(conv_w[0, j])
    vecs[:, 160:176] = _fv(conv_b[0]); vecs[:, 176:192] = _fv(b_gate_a[0]); vecs[:, 192:208] = _fv(b_gate_x[0])
    vecs[:, 208:224] = _fv(lam_rec[0])
    vecs[:, 224] = np.asarray(g_q[0], np.float32); vecs[:, 225] = np.asarray(g_k[0], np.float32)
    vecs[:, 226:228] = np.asarray(g_sub[0], np.float32).reshape(2, 128).T
    lamv = np.concatenate([np.asarray(a[0], np.float32) for a in (lam_q1, lam_k1, lam_q2, lam_k2)])
    lamb = np.ascontiguousarray(np.broadcast_to(lamv[None, :], (128, 512)))
    alibi, tri = _consts()
    shared = {
        "vecs": vecs, "lamb": lamb, "alibi": alibi, "tri": tri,
        "w_qkv": f(w_qkv[0]), "w_o_attn": f(w_o_attn[0]), "w_in_rec": f(w_in_rec[0]),
        "w_gate_a": f(w_gate_a[0]), "w_gate_x": f(w_gate_x[0]), "w_o_rec": f(w_o_rec[0]),
        "w_up": f(w_up), "w_down": f(w_down), "w_ple_proj": f(w_ple_proj), "w_ple_gate": f(w_ple_gate),
    }
    maps = []
    for b in batches:
        for hf in range(2 if split else 1):
            m = dict(shared)
            t0 = hf * S
            m["xT"] = np.ascontiguousarray(np.asarray(x[b, t0:t0 + S], np.float32).T)
            m["pT"] = np.ascontiguousarray(np.asarray(p[:, b, t0:t0 + S], np.float32).transpose(0, 2, 1))
            oc = np.ones((128, 32), np.float32)
            vv = vecs.copy()
            vv[:, 228] = float(hf)
            m["vecs"] = vv
            m["alibic"] = alibi if (hf == 1 or not split) else (alibi - np.float32(30000.0)).astype(np.float32)
            m["onescol"] = oc
            maps.append(m)
    return maps


def kernel(**inputs):
    x = inputs["x"]
    B, S, _ = x.shape
    SH = S // 2
    n_cores = 2 * B
    nc, _P = build(SH, S_ctx=SH, n_cores=n_cores)
    maps = make_in_maps(SH, list(range(B)), split=True, **inputs)
    res = run_bass_kernel_spmd(nc, maps, core_ids=list(range(n_cores)))
    out = np.empty((B, S, D), np.float32)
    for b in range(B):
        for hf in range(2):
            out[b, hf * SH:(hf + 1) * SH] = np.asarray(res.results[2 * b + hf]["outT"], np.float32).T
    return out
```

```python
import math
import numpy as np
import concourse.bass as bass
import concourse.mybir as mybir
from concourse.bass_utils import run_bass_kernel_spmd

F32 = mybir.dt.float32
BF16 = mybir.dt.bfloat16
AF = mybir.ActivationFunctionType
ALU = mybir.AluOpType
AX = mybir.AxisListType

PE, ACT, DVE, POOL, SP = "tensor", "scalar", "vector", "gpsimd", "sync"
ENGINES = [PE, ACT, DVE, POOL, SP]

D = 2048
NCH = 16
NHEAD = 8
DFF = 8192
EPS = 1e-6
T = 1024
TW = 512
NT = T // TW
SCALE = 128 ** -0.5
LAM0 = 0.8 - 0.6 * math.exp(-0.3 * 0)
NV = 232


class Op:
    __slots__ = ("eng", "fn", "reads", "writes", "is_dma", "deps", "signal",
                 "token", "idx", "dma_sem", "pre_waits", "is_cc")

    def __init__(self, eng, fn, reads, writes, is_dma):
        self.eng = eng
        self.fn = fn
        self.reads = reads
        self.writes = writes
        self.is_dma = is_dma
        self.deps = []
        self.signal = False
        self.token = None
        self.dma_sem = None
        self.pre_waits = []
        self.is_cc = False


class Rec:
    def __init__(self):
        self.calls = []

    def __getattr__(self, name):
        def m(*a, **k):
            self.calls.append((name, a, k))
            return self
        return m


def _replay(calls, eng):
    last = None
    for name, a, k in calls:
        last = getattr(eng, name)(*a, **k)
    return last


class Prog:
    def __init__(self, nc, n_dma_sems=32):
        self.nc = nc
        self.ops = []
        self.last_writer = {}
        self.readers = {}
        self.n_dma_sems = n_dma_sems

    def op(self, eng, fn, reads=(), writes=(), dma=False):
        rec = Rec()
        fn(rec)
        calls = rec.calls
        assert calls, "empty op"
        o = Op(eng, (lambda e, calls=calls: _replay(calls, e)), tuple(reads), tuple(writes), dma)
        o.idx = len(self.ops)
        deps = set()
        for k in o.reads:
            w = self.last_writer.get(k)
            if w is not None:
                deps.add(w)
        for k in o.writes:
            w = self.last_writer.get(k)
            if w is not None:
                deps.add(w)
            for r in self.readers.get(k, ()):
                deps.add(r)
        deps.discard(o.idx)
        o.deps = sorted(deps)
        for k in o.writes:
            self.last_writer[k] = o.idx
            self.readers[k] = []
        for k in o.reads:
            if k not in o.writes:
                self.readers.setdefault(k, []).append(o.idx)
        self.ops.append(o)
        return o

    def dma(self, eng, fn, reads=(), writes=()):
        return self.op(eng, fn, reads, writes, dma=True)

    def cc(self, eng, fn, reads=(), writes=()):
        o = self.op(eng, fn, reads, writes, dma=True)
        o.is_cc = True
        return o

    def emit(self, final_wait_keys=()):
        nc = self.nc
        ops = self.ops
        final_ops = set()
        for k in final_wait_keys:
            w = self.last_writer.get(k)
            if w is not None:
                final_ops.add(w)
        for o in ops:
            for d in o.deps:
                p = ops[d]
                if p.is_dma:
                    continue
                if p.eng == PE and o.eng == PE and not o.is_dma:
                    continue
                p.signal = True
        for d in final_ops:
            if not ops[d].is_dma:
                ops[d].signal = True
        eng_sem = {e: nc.alloc_semaphore("S_" + e) for e in ENGINES}
        dma_sems = [nc.alloc_semaphore("D_%d" % i) for i in range(self.n_dma_sems)]
        eng_cnt = {e: 0 for e in ENGINES}
        dma_cnt = [0] * self.n_dma_sems
        dma_last = [None] * self.n_dma_sems
        dma_rr = 0
        dma_rr_sw = 0
        cc_sem = None
        cc_cnt = 0
        for o in ops:
            if o.is_cc:
                cc_sem = nc.alloc_semaphore("CC%d" % cc_cnt)
                o.dma_sem = cc_sem
                o.token = (("C", cc_cnt), cc_sem, 1)
                cc_cnt += 1
            elif o.is_dma:
                half = self.n_dma_sems // 2
                if o.eng == POOL:
                    k = half + dma_rr_sw
                    dma_rr_sw = (dma_rr_sw + 1) % half
                else:
                    k = dma_rr
                    dma_rr = (dma_rr + 1) % half
                if dma_last[k] is not None:
                    o.pre_waits.append(ops[dma_last[k]].token)
                dma_cnt[k] += 16
                dma_last[k] = o.idx
                o.dma_sem = dma_sems[k]
                o.token = (("D", k), dma_sems[k], dma_cnt[k])
            elif o.signal:
                eng_cnt[o.eng] += 1
                o.token = (("E", o.eng), eng_sem[o.eng], eng_cnt[o.eng])
        waited = {e: {} for e in ENGINES}
        per_eng = {e: [] for e in ENGINES}
        for o in ops:
            m = {}
            toks = [ops[d].token for d in o.deps] + list(o.pre_waits)
            for t in toks:
                if t is None:
                    continue
                key, sem, val = t
                if key == ("E", PE) and o.eng == PE and not o.is_dma:
                    continue
                if waited[o.eng].get(key, 0) >= val:
                    continue
                waited[o.eng][key] = val
                if key not in m or m[key][1] < val:
                    m[key] = (sem, val)
            per_eng[o.eng].append((o, list(m.values())))
        final_tokens = [ops[d].token for d in sorted(final_ops)]

        def run_engine(ename, eng):
            for o, waits in per_eng[ename]:
                for sem, val in waits:
                    eng.wait_ge(sem, val)
                last = o.fn(eng)
                if o.is_cc:
                    last.then_inc(o.dma_sem, 1)
                elif o.is_dma:
                    last.then_inc(o.dma_sem, 16)
                elif o.signal:
                    last.then_inc(eng_sem[ename], 1)
            if ename == SP:
                for t in final_tokens:
                    eng.wait_ge(t[1], t[2])

        with nc.Block() as block:
            @block.tensor
            def _(e):
                run_engine(PE, e)

            @block.scalar
            def _(e):
                run_engine(ACT, e)

            @block.vector
            def _(e):
                run_engine(DVE, e)

            @block.gpsimd
            def _(e):
                run_engine(POOL, e)

            @block.sync
            def _(e):
                run_engine(SP, e)
        self.stats = dict(n_ops=len(ops), eng_cnt=dict(eng_cnt),
                          per_eng={e: len(v) for e, v in per_eng.items()})


def build(S, n_layers=2, S_ctx=0, n_cores=4):
    NPASS = S // T
    NCTX = S_ctx // T
    SK = S + S_ctx
    EXCH = S_ctx > 0
    nc = bass.Bass("TRN2", target_bir_lowering=False)

    def din(name, shape, dt=F32):
        return nc.dram_tensor(name, list(shape), dt, kind="ExternalInput").ap()

    xT = din("xT", [D, S])
    onescol_d = din("onescol", [128, 32])
    pT = din("pT", [2, 256, S])
    vecs_d = din("vecs", [128, NV])
    lam_d = din("lamb", [128, 512])
    alibi_d = din("alibi", [128, 256])
    tri_d = din("tri", [128, 128])
    w_qkv = din("w_qkv", [D, 6144])
    w_oa = din("w_o_attn", [D, D])
    w_in = din("w_in_rec", [D, 4096])
    w_ga = din("w_gate_a", [8, 256, 256])
    w_gx = din("w_gate_x", [8, 256, 256])
    w_or = din("w_o_rec", [D, D])
    w_up = din("w_up", [2, D, DFF])
    w_down = din("w_down", [2, DFF, D])
    w_pp = din("w_ple_proj", [2, 256, D])
    w_pg = din("w_ple_gate", [2, D, D])
    outT = nc.dram_tensor("outT", [D, S], F32, kind="ExternalOutput").ap()
    assert EXCH and S_ctx == S, "sequence-split mode only"
    alibic_d = din("alibic", [128, 256])
    Kc = [[nc.dram_tensor("Kc_%d_%d" % (p_, hh), [1024, T], BF16).ap() for hh in range(2)] for p_ in range(NPASS)]
    Vc = [[nc.dram_tensor("Vc_%d_%d" % (p_, hh), [T // 2, D], BF16).ap() for hh in range(2)] for p_ in range(NPASS)]
    KGc = [[nc.dram_tensor("KG_%d_%d" % (p_, hh), [2 * 1024, T], BF16).ap() for hh in range(2)] for p_ in range(NPASS)]
    VGc = [[nc.dram_tensor("VG_%d_%d" % (p_, hh), [2 * (T // 2), D], BF16).ap() for hh in range(2)] for p_ in range(NPASS)]
    Qscr = nc.dram_tensor("Qscr", [16 * 128, S], BF16).ap()
    if EXCH:
        Hscr = nc.dram_tensor("Hscr", [NPASS, D, T], F32).ap()
        cbounce = nc.dram_tensor("cbounce", [128, 80], F32).ap()
        cgath = nc.dram_tensor("cgath", [256, 80], F32).ap()

    arena = nc.alloc_sbuf_tensor("arena", [128, 16384], F32)
    arena_b = arena.bitcast(BF16)
    h = arena[:].rearrange("p (c n) -> p c n", c=NCH)
    kT = arena_b[:, 0:8192].rearrange("p (c n) -> p c n", c=2)
    Vx = arena_b[:, 8192:8192 + 32 * 258].rearrange("p (j d) -> p j d", d=258)
    qT = arena_b[:, 16448:16448 + 2048].rearrange("p (c n) -> p c n", c=2)
    Pb = [arena_b[:, 18496 + i * 512:18496 + (i + 1) * 512].rearrange("p (c n) -> p c n", c=2)
          for i in range(3)]
    onb = [arena_b[:, 20032 + i * 256:20032 + (i + 1) * 256] for i in range(2)]
    ot = [arena[:, 10272 + i * 256:10272 + (i + 1) * 256] for i in range(4)]

    xn = nc.alloc_sbuf_tensor("xn", [128, NCH, T], BF16)
    act2 = nc.alloc_sbuf_tensor("act2", [128, NCH, T], BF16)
    act2f = act2[:].rearrange("p c n -> p (c n)")
    slabs = [nc.alloc_sbuf_tensor("slab%d" % i, [128, 8192], BF16) for i in range(3)]
    sq_all = nc.alloc_sbuf_tensor("sq_all", [128, 4 * TW], BF16)
    sq = [sq_all[:, i * TW:(i + 1) * TW] for i in range(4)]
    sq_f = sq_all.bitcast(F32)
    sqf = [sq_f[:, i * TW:(i + 1) * TW] for i in range(2)]
    rstd = [nc.alloc_sbuf_tensor("rstd%d" % i, [128, TW], F32) for i in range(2)]
    ybuf = [nc.alloc_sbuf_tensor("ybuf%d" % i, [128, TW], BF16) for i in range(2)]
    xrb = [nc.alloc_sbuf_tensor("xrb%d" % i, [128, TW + 8], F32) for i in range(2)]
    xcf = [nc.alloc_sbuf_tensor("xcf%d" % i, [128, TW], F32) for i in range(2)]
    xcb = [nc.alloc_sbuf_tensor("xcb%d" % i, [128, TW], BF16) for i in range(2)]
    rbuf = nc.alloc_sbuf_tensor("rbuf", [128, TW], F32)
    igbuf = nc.alloc_sbuf_tensor("igbuf", [128, TW], F32)
    sm = nc.alloc_sbuf_tensor("sm", [128, 32], F32)
    onescol = nc.alloc_sbuf_tensor("onescol_sb", [128, 32], F32)
    carry = nc.alloc_sbuf_tensor("carry", [128, 80], F32)
    carry_in = nc.alloc_sbuf_tensor("carry_in", [128, 80], F32)
    hist = nc.alloc_sbuf_tensor("hist", [128, NCH, 4], F32)
    state = nc.alloc_sbuf_tensor("state", [128, NCH], F32)
    c8 = nc.alloc_sbuf_tensor("c8", [128, NCH], F32)
    c8h = nc.alloc_sbuf_tensor("c8h", [128, NCH], F32)
    eps_t = nc.alloc_sbuf_tensor("eps_t", [128, 2], F32)
    gsc = nc.alloc_sbuf_tensor("gsc", [128, 2], F32)
    hb = nc.alloc_sbuf_tensor("hb", [128, 2 * NCH], F32)
    neglam = nc.alloc_sbuf_tensor("neglam", [128, 8], F32)
    vecs = nc.alloc_sbuf_tensor("vecs_sb", [128, NV], F32)
    alibi = nc.alloc_sbuf_tensor("alibi_sb", [128, 256], F32)
    alibic = nc.alloc_sbuf_tensor("alibic_sb", [128, 256], F32)
    lam_sb = nc.alloc_sbuf_tensor("lam_sb", [128, 512], F32)
    tri = nc.alloc_sbuf_tensor("tri_sb", [128, 128], BF16)
    ident = nc.alloc_sbuf_tensor("ident", [128, 128], BF16)
    ones = nc.alloc_sbuf_tensor("ones", [128, 128], BF16)

    ps = [nc.alloc_psum_tensor("ps%d" % i, [128, 512], F32) for i in range(7)]
    pst = nc.alloc_psum_tensor("pst", [128, 1024], BF16)

    P = Prog(nc)

    G_MIX = [0, 48]
    G_MLP = [16, 64]
    G_PLE = [32, 80]
    CONVW = 96
    CONVB = 160
    BGA = 176
    BGX = 192
    LAMR = 208
    GQ = 224
    GK = 225

    st = dict(gp=0, slab=0, sq=0, rstd=0, stage=0, dq=0)
    slab_gen = [0, 0, 0]

    def gp_bank():
        b = st["gp"]
        st["gp"] = (b + 1) % 7
        return b

    def next_sq():
        i = st["sq"]
        st["sq"] = (i + 1) % 4
        return i

    def next_rstd():
        i = st["rstd"]
        st["rstd"] = (i + 1) % 2
        return i

    def next_stage():
        i = st["stage"]
        st["stage"] = (i + 1) % 16
        return i

    def hk(c, tt):
        return ("h", c, tt)

    def xk(c, tt):
        return ("xn", c, tt)

    def ak(c, tt):
        return ("act2", c, tt)

    def tok(tt):
        return slice(tt * TW, (tt + 1) * TW)

    def slab_load(parts):
        s = st["slab"]
        st["slab"] = (s + 1) % 3
        slab_gen[s] += 1
        keys = []
        for i, (dv, src) in enumerate(parts):
            k = ("slab", s, i)
            keys.append(k)
            dst = dv(slabs[s])
            P.dma(POOL, lambda e, dst=dst, src=src: e.dma_start(out=dst, in_=src), writes=[k])
        return dict(s=s, gen=slab_gen[s], keys=keys, t=slabs[s])

    def use(hd):
        assert slab_gen[hd["s"]] == hd["gen"], "slab slot reclaimed while live"
        return hd["keys"]

    def col_slab(W, c0, w, nk=16):
        src = W.rearrange("(kc p) n -> p kc n", p=128)
        half = nk // 2
        parts = []
        for i in range(2):
            k0, k1 = i * half, (i + 1) * half
            parts.append((lambda t, k0=k0, k1=k1: t[:, 0:nk * w].rearrange("p (k n) -> p k n", k=nk)[:, k0:k1, :],
                          src[:, k0:k1, c0:c0 + w]))
        hd = slab_load(parts)
        hd["v"] = hd["t"][:, 0:nk * w].rearrange("p (k n) -> p k n", k=nk)
        return hd

    def row_slab(W, r0, nk, ncols):
        src = W[r0:r0 + nk * 128, :].rearrange("(kc p) n -> p kc n", p=128)
        half = max(1, nk // 2)
        parts = []
        for i in range(nk // half):
            k0, k1 = i * half, (i + 1) * half
            parts.append((lambda t, k0=k0, k1=k1: t[:, 0:nk * ncols].rearrange("p (k n) -> p k n", k=nk)[:, k0:k1, :],
                          src[:, k0:k1, :]))
        hd = slab_load(parts)
        hd["v"] = hd["t"][:, 0:nk * ncols].rearrange("p (k n) -> p k n", k=nk)
        return hd

    def mm_group(out_ap, bank, lhs_fn, rhs_fn, nk, reads):
        def f(e):
            last = None
            for kc in range(nk):
                last = e.matmul(out_ap, lhsT=lhs_fn(kc), rhs=rhs_fn(kc), start=(kc == 0), stop=(kc == nk - 1))
            return last
        P.op(PE, f, reads=reads, writes=[("ps", bank)])

    P.dma(SP, lambda e: e.dma_start(out=vecs[:], in_=vecs_d), writes=["vecs"])
    P.dma(SP, lambda e: e.dma_start(out=lam_sb[:], in_=lam_d), writes=["lam_sb"])
    P.dma(SP, lambda e: e.dma_start(out=alibi[:], in_=alibi_d), writes=["alibi"])
    P.dma(SP, lambda e: e.dma_start(out=alibic[:], in_=alibic_d), writes=["alibic"])
    P.dma(POOL, lambda e: e.dma_start(out=tri[:], in_=tri_d), writes=["tri"])
    P.dma(SP, lambda e: e.dma_start(out=onescol[:], in_=onescol_d), writes=["onescol"])

    P.op(POOL, lambda e: e.memset(ident[:], 1.0), writes=["ident"])
    P.op(POOL, lambda e: e.affine_select(out=ident[:], in_=ident[:], pattern=[[-1, 128]], compare_op=ALU.is_equal,
                                         fill=0.0, base=0, channel_multiplier=1), reads=["ident"], writes=["ident"])
    P.op(DVE, lambda e: e.memset(ones[:], 1.0), writes=["ones"])
    P.op(DVE, lambda e: e.memset(eps_t[:], float(EPS)), writes=["eps"])
    P.op(DVE, lambda e: e.tensor_scalar_mul(out=gsc[:], in0=vecs[:, 226:228], scalar1=float(1.0 - LAM0)),
         reads=["vecs"], writes=["gsc"])
    P.op(DVE, lambda e: e.memset(hist[:], 0.0), writes=[("hist", cc) for cc in range(NCH)])
    P.op(DVE, lambda e: e.memset(state[:], 0.0), writes=[("state", cc) for cc in range(NCH)])
    P.op(DVE, lambda e: e.tensor_tensor(out=rbuf[:, 0:128], in0=lam_sb[:, 0:128], in1=lam_sb[:, 128:256], op=ALU.mult),
         reads=["lam_sb"], writes=["rbuf"])
    P.op(DVE, lambda e: e.tensor_tensor(out=igbuf[:, 0:128], in0=lam_sb[:, 256:384], in1=lam_sb[:, 384:512], op=ALU.mult),
         reads=["lam_sb"], writes=["igbuf"])
    P.op(DVE, lambda e: e.reduce_sum(out=neglam[:, 0:1], in_=rbuf[:, 0:128], axis=AX.X),
         reads=["rbuf"], writes=["nl0"])
    P.op(DVE, lambda e: e.reduce_sum(out=neglam[:, 1:2], in_=igbuf[:, 0:128], axis=AX.X),
         reads=["igbuf"], writes=["nl01"])
    P.op(ACT, lambda e: e.activation(out=neglam[:, 2:4], in_=neglam[:, 0:2], func=AF.Exp), reads=["nl0", "nl01"], writes=["nl23"])
    P.op(DVE, lambda e: e.tensor_tensor(out=neglam[:, 4:5], in0=neglam[:, 3:4], in1=neglam[:, 2:3], op=ALU.subtract),
         reads=["nl23"], writes=["nl4"])
    P.op(DVE, lambda e: e.tensor_scalar_add(out=neglam[:, 5:6], in0=neglam[:, 4:5], scalar1=float(-LAM0)),
         reads=["nl4"], writes=["neglam"])
    NEGLAM = neglam[:, 5:6]
    P.op(ACT, lambda e: e.activation(out=c8[:], in_=vecs[:, LAMR:LAMR + 16], func=AF.Exp, scale=-1.0),
         reads=["vecs"], writes=["c8a"])
    P.op(ACT, lambda e: e.activation(out=c8[:], in_=c8[:], func=AF.Ln, bias=1.0), reads=["c8a"], writes=["c8b"])
    P.op(DVE, lambda e: e.tensor_scalar_mul(out=c8[:], in0=c8[:], scalar1=-8.0), reads=["c8b"], writes=["c8"])
    P.op(DVE, lambda e: e.tensor_scalar_mul(out=c8h[:], in0=c8[:], scalar1=0.5), reads=["c8"], writes=["c8h"])
    P.op(DVE, lambda e: e.tensor_scalar_mul(out=hb[:], in0=vecs[:, BGA:BGA + 2 * NCH], scalar1=0.5), reads=["vecs"], writes=["hb"])

    KT_ALL = [("kT", c, w, pp) for c in range(2) for w in ("ctx", "own") for pp in range(NPASS)]
    VX_ALL = [("Vx", w, pp, vh) for w in ("ctx", "own") for pp in range(NPASS) for vh in range(2)]
    ARENA_ATT_KEYS = KT_ALL + VX_ALL + [("qT", 0, 0), ("qT", 0, 1), ("qT", 1, 0), ("qT", 1, 1)] + \
        [("Pb", i) for i in range(3)] + [("onb", i) for i in range(2)] + [("ot", i) for i in range(4)]
    H_KEYS = [hk(c, tt) for c in range(NCH) for tt in range(NT)]

    def rms_stat(src_ap_fn, src_keys_fn, nsrc, inv_n, sq_eng=None):
        bank = gp_bank()
        for c in range(nsrc):
            si = next_sq()
            if sq_eng is not None and sq_eng(c) == DVE:
                P.op(DVE, lambda e, c=c, si=si: e.tensor_tensor(out=sq[si][:], in0=src_ap_fn(c), in1=src_ap_fn(c), op=ALU.mult),
                     reads=src_keys_fn(c), writes=[("sq", si)])
            else:
                P.op(ACT, lambda e, c=c, si=si: e.activation(out=sq[si][:], in_=src_ap_fn(c), func=AF.Square),
                     reads=src_keys_fn(c), writes=[("sq", si)])
            P.op(PE, lambda e, c=c, si=si: e.matmul(ps[bank][:], lhsT=ones[:], rhs=sq[si][:],
                                                     start=(c == 0), stop=(c == nsrc - 1)),
                 reads=[("sq", si), "ones"], writes=[("ps", bank)])
        ri = next_rstd()
        P.op(ACT, lambda e: e.activation(out=rstd[ri][:], in_=ps[bank][:], func=AF.Sqrt, bias=eps_t[:, 0:1], scale=float(inv_n)),
             reads=[("ps", bank), "eps"], writes=[("rstd", ri)])
        P.op(DVE, lambda e: e.reciprocal(out=rstd[ri][:], in_=rstd[ri][:]),
             reads=[("rstd", ri)], writes=[("rstd", ri)])
        return ri

    def rmsnorm_h(gcol):
        for tt in range(NT):
            ri = rms_stat(lambda c: h[:, c, tok(tt)], lambda c: [hk(c, tt)], NCH, 1.0 / D,
                          sq_eng=(lambda c: DVE if c % 2 else ACT) if tt == 0 else None)
            for c in range(NCH):
                P.op(DVE, lambda e, c=c: e.scalar_tensor_tensor(out=xn[:, c, tok(tt)], in0=h[:, c, tok(tt)],
                                                                scalar=vecs[:, gcol + c:gcol + c + 1], in1=rstd[ri][:],
                                                                op0=ALU.mult, op1=ALU.mult),
                     reads=[hk(c, tt), ("rstd", ri), "vecs"], writes=[xk(c, tt)])

    def load_h(pos, src=None, reads=()):
        src = xT if src is None else src
        for i in range(4):
            P.dma(SP, lambda e, i=i: e.dma_start(out=h[:, 4 * i:4 * i + 4, :],
                                                 in_=src.rearrange("(c p) n -> p c n", p=128)[:, 4 * i:4 * i + 4, pos:pos + T]),
                  reads=list(reads), writes=[hk(c, tt) for c in range(4 * i, 4 * i + 4) for tt in range(NT)])

    def qk_norm_evac(bank, gcolumn, out_ap, out_keys, defer=None):
        si = next_sq()
        P.op(ACT, lambda e: e.activation(out=sq[si][:], in_=ps[bank][:], func=AF.Square),
             reads=[("ps", bank)], writes=[("sq", si)])
        if defer is not None:
            defer.append(lambda: _qk_norm_tail(bank, si, gcolumn, out_ap, out_keys))
        else:
            _qk_norm_tail(bank, si, gcolumn, out_ap, out_keys)

    def _qk_norm_tail(bank, si, gcolumn, out_ap, out_keys):
        b2 = gp_bank()
        P.op(PE, lambda e: e.matmul(ps[b2][:], lhsT=ones[:], rhs=sq[si][:], start=True, stop=True),
             reads=[("sq", si), "ones"], writes=[("ps", b2)])
        ri = next_rstd()
        P.op(ACT, lambda e: e.activation(out=rstd[ri][:], in_=ps[b2][:], func=AF.Sqrt, bias=eps_t[:, 0:1], scale=1.0 / 128),
             reads=[("ps", b2), "eps"], writes=[("rstd", ri)])
        P.op(DVE, lambda e: e.reciprocal(out=rstd[ri][:], in_=rstd[ri][:]),
             reads=[("rstd", ri)], writes=[("rstd", ri)])
        P.op(DVE, lambda e: e.scalar_tensor_tensor(out=out_ap, in0=ps[bank][:], scalar=vecs[:, gcolumn:gcolumn + 1],
                                                   in1=rstd[ri][:], op0=ALU.mult, op1=ALU.mult),
             reads=[("ps", bank), ("rstd", ri), "vecs"], writes=out_keys)

    def qkv_phase(p, rgp, late=None):
        pos = p * T
        xkeys = lambda tt: [xk(c, tt) for c in range(NCH)]

        def gather(which):
            for hh in range(2):
                if which == "K":
                    P.cc(POOL, lambda e, hh=hh: e.collective_compute("AllGather", ALU.bypass, replica_groups=rgp,
                                                                     ins=[Kc[p][hh]], outs=[KGc[p][hh]]),
                         reads=[("Kc", idx, p, tt) for idx in range(8 * hh, 8 * hh + 8) for tt in range(NT)],
                         writes=[("KG", p, hh)])
                else:
                    P.cc(POOL, lambda e, hh=hh: e.collective_compute("AllGather", ALU.bypass, replica_groups=rgp,
                                                                     ins=[Vc[p][hh]], outs=[VGc[p][hh]]),
                         reads=[("Vc", s_, p, tb) for s_ in range(4) for tb in range(4 * hh, 4 * hh + 4)],
                         writes=[("VG", p, hh)])

        def qk_part(c0, gcol, kname, after_first=None, late=None):
            tails = []
            for s_ in range(4):
                sl = col_slab(w_qkv, c0 + 512 * s_, 512)
                for m in range(4):
                    idx = 4 * s_ + m
                    for tt in range(NT):
                        bank = gp_bank()
                        mm_group(ps[bank][:], bank, lambda kc: sl["v"][:, kc, m * 128:(m + 1) * 128],
                                 lambda kc: xn[:, kc, tok(tt)], NCH, use(sl) + xkeys(tt))
                        sg = next_stage()
                        dst = act2[:, sg, 0:TW]
                        prev_tail = list(tails)
                        del tails[:]
                        qk_norm_evac(bank, gcol, dst, [ak(sg, 0)], defer=tails)
                        if kname == "Kc":
                            r0 = (idx % 8) * 128
                            out_ap = Kc[p][idx // 8][r0:r0 + 128, tt * TW:(tt + 1) * TW]
                        else:
                            out_ap = Qscr[idx * 128:(idx + 1) * 128, pos + tt * TW:pos + (tt + 1) * TW]
                        tails.append(lambda dst=dst, out_ap=out_ap, sg=sg, idx=idx, tt=tt: P.dma(
                            SP, lambda e: e.dma_start(out=out_ap, in_=dst),
                            reads=[ak(sg, 0)], writes=[(kname, idx, p, tt)]))
                        for f_ in prev_tail:
                            f_()
                if s_ == 1 and after_first is not None:
                    after_first()
                if s_ == 2 and late is not None:
                    late()
            for f_ in tails:
                f_()
            del tails[:]

        def v_part(after_first=None):
            for s_ in range(4):
                sl = col_slab(w_qkv, 4096 + 512 * s_, 512)
                for tb in range(T // 128):
                    bank = gp_bank()
                    mm_group(ps[bank][:], bank, lambda kc: xn[:, kc, tb * 128:(tb + 1) * 128],
                             lambda kc: sl["v"][:, kc, :], NCH, use(sl) + xkeys(tb // 4))
                    sg = next_stage()
                    dst = act2[:, sg, 0:TW]
                    P.op(ACT, lambda e, dst=dst, bank=bank: e.activation(out=dst, in_=ps[bank][:], func=AF.Copy),
                         reads=[("ps", bank)], writes=[ak(sg, 0)])
                    vr0 = (tb % 4) * 128
                    P.dma(SP, lambda e, dst=dst, tb=tb, s_=s_, vr0=vr0: e.dma_start(
                        out=Vc[p][tb // 4][vr0:vr0 + 128, s_ * 512:(s_ + 1) * 512], in_=dst),
                        reads=[ak(sg, 0)], writes=[("Vc", s_, p, tb)])
                if s_ == 1 and after_first is not None:
                    after_first()

        qk_part(2048, GK, "Kc")
        v_part(after_first=lambda: gather("K"))
        qk_part(0, GQ, "Qscr", after_first=lambda: gather("V"), late=late)

    def att_loads(p_own, hd):
        nown = (p_own + 1) * T
        ncb = S_ctx // 128
        for c in range(2):
            idx = 2 * hd + c
            kh, r0 = idx // 8, (idx % 8) * 128
            for pp in range(NCTX):
                P.dma(SP, lambda e, c=c, pp=pp: e.dma_start(out=kT[:, c, pp * T:(pp + 1) * T], in_=KGc[pp][kh][r0:r0 + 128, :]),
                      reads=[("KG", pp, kh)], writes=[("kT", c, "ctx", pp)])
            for pp in range(p_own + 1):
                P.dma(SP, lambda e, c=c, pp=pp: e.dma_start(out=kT[:, c, S_ctx + pp * T:S_ctx + (pp + 1) * T],
                                                            in_=Kc[pp][kh][r0:r0 + 128, :]),
                      reads=[("Kc", idx, pp, tt) for tt in range(NT)], writes=[("kT", c, "own", pp)])
            P.dma(SP, lambda e, c=c: e.dma_start(out=qT[:, c, :], in_=Qscr[idx * 128:(idx + 1) * 128, p_own * T:(p_own + 1) * T]),
                  reads=[("Qscr", idx, p_own, tt) for tt in range(NT)],
                  writes=[("qT", c, 0), ("qT", c, 1)])
        for pp in range(NCTX):
            for vh in range(2):
                b0 = pp * 8 + vh * 4
                P.dma(SP, lambda e, pp=pp, vh=vh, b0=b0: e.dma_start(
                    out=Vx[:, b0:b0 + 4, 0:256],
                    in_=VGc[pp][vh][0:T // 2, hd * 256:(hd + 1) * 256].rearrange("(j q) d -> q j d", q=128)),
                    reads=[("VG", pp, vh)], writes=[("Vx", "ctx", pp, vh)])
        for pp in range(p_own + 1):
            for vh in range(2):
                b0 = ncb + pp * 8 + vh * 4
                P.dma(SP, lambda e, pp=pp, vh=vh, b0=b0: e.dma_start(
                    out=Vx[:, b0:b0 + 4, 0:256],
                    in_=Vc[pp][vh][:, hd * 256:(hd + 1) * 256].rearrange("(j q) d -> q j d", q=128)),
                    reads=[("Vc", hd // 2, pp, tb) for tb in range(4 * vh, 4 * vh + 4)], writes=[("Vx", "own", pp, vh)])

    def att_prefetch(p_own):
        P.op(DVE, lambda e: e.tensor_copy(out=Vx[:, :, 256:257], in_=onescol[:, :].rearrange("p (j o) -> p j o", o=1)),
             reads=["onescol"], writes=H_KEYS + ARENA_ATT_KEYS)
        att_loads(p_own, 0)

    def attention(p_own, prefetched=False):
        p = p_own + NCTX
        pos = p * T
        nk = (p + 1) * T
        nown = (p_own + 1) * T
        ncb = S_ctx // 128
        if not prefetched:
            att_prefetch(p_own)
        for hd in range(NHEAD):
            if hd > 0:
                att_loads(p_own, hd)
            def kpiece(j):
                w, jj = ("ctx", j) if j < ncb else ("own", j - ncb)
                return [("kT", 0, w, jj // 8), ("kT", 1, w, jj // 8)]

            def vpiece(j):
                w, jj = ("ctx", j) if j < ncb else ("own", j - ncb)
                return [("Vx", w, jj // 8, (jj % 8) // 4)]
            acc = [[0, 1], [2, 3]]
            pending = []

            def flush_pending():
                while pending:
                    pending.pop(0)()

            def _part1(qb, hd, q0l, tt):
                a0, a1 = acc[qb]
                o0 = ot[qb]
                ob = onb[qb]
                sc = 8 * qb
                kk = lambda i, qb=qb: ("sm", qb, i)
                P.op(DVE, lambda e: e.reciprocal(out=sm[:, sc:sc + 1], in_=ps[a0][:, 256:257]),
                     reads=[("ps", a0)], writes=[kk(0)])
                P.op(DVE, lambda e: e.reciprocal(out=sm[:, sc + 1:sc + 2], in_=ps[a1][:, 256:257]),
                     reads=[("ps", a1)], writes=[kk(1)])
                P.op(DVE, lambda e: e.tensor_tensor(out=sm[:, sc + 2:sc + 3], in0=sm[:, sc + 1:sc + 2], in1=NEGLAM,
                                                    op=ALU.mult), reads=[kk(1), "neglam"], writes=[kk(2)])
                P.op(DVE, lambda e: e.tensor_scalar(out=o0, in0=ps[a0][:, 0:256], scalar1=sm[:, sc:sc + 1], scalar2=None,
                                                    op0=ALU.mult), reads=[("ps", a0), kk(0)], writes=[("ot", qb)])
                P.op(DVE, lambda e: e.scalar_tensor_tensor(out=ob, in0=ps[a1][:, 0:256], scalar=sm[:, sc + 2:sc + 3], in1=o0,
                                                           op0=ALU.mult, op1=ALU.add),
                     reads=[("ps", a1), kk(2), ("ot", qb)], writes=[("onb", qb)])

            for t in range(T // 256):
                Q0 = pos + 256 * t
                q0l = 256 * t
                jb = Q0 // 128
                jlast = jb + 1
                tt = q0l // TW

                def qk(j):
                    m = jb - j
                    qa = 128 if m == -1 else 0
                    sb = 4 + (j % 3)
                    sv = ps[sb][:].rearrange("p (c n) -> p c n", c=2)

                    def f(e):
                        last = None
                        for c in range(2):
                            last = e.matmul(sv[:, c, qa:256], lhsT=kT[:, c, j * 128:(j + 1) * 128],
                                            rhs=qT[:, c, q0l + qa:q0l + 256], start=True, stop=True)
                        return last
                    P.op(PE, f, reads=kpiece(j) + [("qT", 0, tt), ("qT", 1, tt)], writes=[("ps", sb)])
                    pi = j % 3
                    col = hd * 32 + (m + 1)
                    atab = alibic if j < ncb else alibi
                    P.op(ACT, lambda e: e.activation(out=Pb[pi][:, :, qa:256], in_=sv[:, :, qa:256], func=AF.Exp,
                                                     bias=atab[:, col:col + 1], scale=float(SCALE)),
                         reads=[("ps", sb), "alibi", "alibic"], writes=[("Pb", pi)])
                    if m == 0 or m == -1:
                        lo = 0 if m == 0 else 128

                        def fm(e):
                            last = None
                            for c in range(2):
                                last = e.tensor_tensor(out=Pb[pi][:, c, lo:lo + 128], in0=Pb[pi][:, c, lo:lo + 128],
                                                       in1=tri[:], op=ALU.mult)
                            return last
                        P.op(DVE, fm, reads=[("Pb", pi), "tri"], writes=[("Pb", pi)])

                def pv(j):
                    m = jb - j
                    pi = j % 3
                    for qb in range(2):
                        if m == -1 and qb == 0:
                            continue
                        lastj = jb if qb == 0 else jb + 1

                        def f(e, qb=qb, lastj=lastj):
                            last = None
                            for c in range(2):
                                last = e.matmul(ps[acc[qb][c]][:, 0:257], lhsT=Pb[pi][:, c, qb * 128:(qb + 1) * 128],
                                                rhs=Vx[:, j, 0:257], start=(j == 0), stop=(j == lastj))
                            return last
                        P.op(PE, f, reads=[("Pb", pi)] + vpiece(j), writes=[("ps", acc[qb][0]), ("ps", acc[qb][1])])

                qk(0)
                qk(1)
                def part1(qb):
                    return _part1(qb, hd, q0l, tt)

                for j in range(jlast + 1):
                    if j + 2 <= jlast:
                        qk(j + 2)
                    pv(j)
                    if j == 2:
                        flush_pending()
                    if j == jb:
                        part1(0)
                flush_pending()
                part1(1)
                def part2(qb, q0l=q0l, tt=tt):
                    ob = onb[qb]
                    tsl = slice(q0l + qb * 128, q0l + (qb + 1) * 128)

                    def ftr(e):
                        last = None
                        for half in range(2):
                            last = e.transpose(pst[:, half * 128:(half + 1) * 128], ob[:, half * 128:(half + 1) * 128], ident[:])
                        return last
                    P.op(PE, ftr, reads=[("onb", qb), "ident"], writes=[("pst",)])
                    P.op(DVE, lambda e: e.tensor_copy(out=act2[:, 2 * hd:2 * hd + 2, tsl],
                                                      in_=pst[:, 0:256].rearrange("p (a b) -> p a b", a=2)),
                         reads=[("pst",)], writes=[ak(2 * hd, tt), ak(2 * hd + 1, tt)])
                pending.append(lambda: part2(0))
                pending.append(lambda: part2(1))
            flush_pending()
            for tt in range(NT):
                ri = rms_stat(lambda c: act2[:, 2 * hd + c, tok(tt)], lambda c: [ak(2 * hd + c, tt)], 2, 1.0 / 256)
                for c in range(2):
                    P.op(DVE, lambda e, c=c: e.scalar_tensor_tensor(out=act2[:, 2 * hd + c, tok(tt)], in0=act2[:, 2 * hd + c, tok(tt)],
                                                                    scalar=gsc[:, c:c + 1], in1=rstd[ri][:],
                                                                    op0=ALU.mult, op1=ALU.mult),
                         reads=[ak(2 * hd + c, tt), ("rstd", ri), "gsc"], writes=[ak(2 * hd + c, tt)])
        P.op(DVE, lambda e: e.memset(sm[:, 31:32], 0.0), reads=[], writes=H_KEYS + ARENA_ATT_KEYS)

    def proj_add_h(W, in_buf, in_key):
        for s in range(4):
            sl = col_slab(W, 512 * s, 512)
            order = [(m, tt) for m in range(4) for tt in range(NT)] if s < 3 else [(m, tt) for tt in range(NT) for m in range(4)]
            for (m, tt) in order:
                oc = 4 * s + m
                if True:
                    bank = gp_bank()
                    mm_group(ps[bank][:], bank, lambda kc: sl["v"][:, kc, m * 128:(m + 1) * 128],
                             lambda kc: in_buf[:, kc, tok(tt)], NCH, use(sl) + [in_key(kc, tt) for kc in range(NCH)])
                    P.op(DVE, lambda e, oc=oc, tt=tt, bank=bank: e.tensor_tensor(out=h[:, oc, tok(tt)], in0=h[:, oc, tok(tt)],
                                                                                 in1=ps[bank][:], op=ALU.add),
                         reads=[("ps", bank), hk(oc, tt)], writes=[hk(oc, tt)])

    def mlp(l):
        rmsnorm_h(G_MLP[l])
        NS = DFF // 512
        wu = [None] * NS
        wd = [None] * NS

        def up(s):
            sl = wu[s]
            hb = s % 2
            for m in range(4):
                for tt in range(NT):
                    bank = gp_bank()
                    mm_group(ps[bank][:], bank, lambda kc: sl["v"][:, kc, m * 128:(m + 1) * 128],
                             lambda kc: xn[:, kc, tok(tt)], NCH, use(sl) + [xk(kc, tt) for kc in range(NCH)])
                    dst = act2[:, 4 * hb + m, tok(tt)]
                    P.op(ACT, lambda e, dst=dst, bank=bank: e.activation(out=dst, in_=ps[bank][:], func=AF.Relu),
                         reads=[("ps", bank)], writes=[ak(4 * hb + m, tt)])
                    P.op(ACT, lambda e, dst=dst: e.activation(out=dst, in_=dst, func=AF.Square),
                         reads=[ak(4 * hb + m, tt)], writes=[ak(4 * hb + m, tt)])

        def down(s):
            sl = wd[s]
            hb = s % 2
            order = [(oc, tt) for oc in range(NCH) for tt in range(NT)] if s < NS - 1 else \
                [(oc, tt) for tt in range(NT) for oc in range(NCH)]
            for (oc, tt) in order:
                if True:
                    bank = gp_bank()
                    mm_group(ps[bank][:], bank, lambda kc: sl["v"][:, kc, oc * 128:(oc + 1) * 128],
                             lambda kc: act2[:, 4 * hb + kc, tok(tt)], 4, use(sl) + [ak(4 * hb + kc, tt) for kc in range(4)])
                    P.op(DVE, lambda e, oc=oc, tt=tt, bank=bank: e.tensor_tensor(out=h[:, oc, tok(tt)], in0=h[:, oc, tok(tt)],
                                                                                 in1=ps[bank][:], op=ALU.add),
                         reads=[("ps", bank), hk(oc, tt)], writes=[hk(oc, tt)])

        wu[0] = col_slab(w_up[l], 0, 512)
        up(0)
        for s in range(1, NS):
            wu[s] = col_slab(w_up[l], 512 * s, 512)
            wd[s - 1] = row_slab(w_down[l], 512 * (s - 1), 4, D)
            up(s)
            down(s - 1)
        wd[NS - 1] = row_slab(w_down[l], 512 * (NS - 1), 4, D)
        down(NS - 1)

    def ple(l, p):
        pos = p * T
        rmsnorm_h(G_PLE[l])
        pb = act2[:, 0:2, :]
        P.dma(POOL, lambda e: e.dma_start(out=pb, in_=pT[l].rearrange("(kc q) n -> q kc n", q=128)[:, :, pos:pos + T]),
              writes=[ak(c, tt) for c in range(2) for tt in range(NT)])
        wpp = act2f[:, 2048:2048 + 4096].rearrange("p (k n) -> p k n", k=2)
        P.dma(POOL, lambda e: e.dma_start(out=wpp, in_=w_pp[l].rearrange("(kc q) n -> q kc n", q=128)),
              writes=[ak(c, tt) for c in range(2, 6) for tt in range(NT)])
        wpp_keys = [ak(c, tt) for c in range(2, 6) for tt in range(NT)]
        tb = [rbuf, igbuf]
        tbk = ["rbuf", "igbuf"]
        n = 0
        for s in range(4):
            sl = col_slab(w_pg[l], 512 * s, 512)
            order = [(m, tt) for m in range(4) for tt in range(NT)] if s < 3 else [(m, tt) for tt in range(NT) for m in range(4)]
            for (m, tt) in order:
                oc = 4 * s + m
                if True:
                    bg = gp_bank()
                    mm_group(ps[bg][:], bg, lambda kc: sl["v"][:, kc, m * 128:(m + 1) * 128],
                             lambda kc: xn[:, kc, tok(tt)], NCH, use(sl) + [xk(kc, tt) for kc in range(NCH)])
                    bp = gp_bank()
                    mm_group(ps[bp][:], bp, lambda kc: wpp[:, kc, oc * 128:(oc + 1) * 128],
                             lambda kc: pb[:, kc, tok(tt)], 2, wpp_keys + [ak(0, tt), ak(1, tt)])
                    tbi = n % 2
                    n += 1
                    tbuf, tkey = tb[tbi], tbk[tbi]
                    P.op(ACT, lambda e, tbuf=tbuf, bg=bg: e.activation(out=tbuf[:], in_=ps[bg][:], func=AF.Sigmoid),
                         reads=[("ps", bg)], writes=[tkey])
                    P.op(DVE, lambda e, tbuf=tbuf, bp=bp: e.tensor_tensor(out=tbuf[:], in0=tbuf[:], in1=ps[bp][:], op=ALU.mult),
                         reads=[tkey, ("ps", bp)], writes=[tkey])
                    P.op(DVE, lambda e, tbuf=tbuf, oc=oc, tt=tt: e.tensor_tensor(out=h[:, oc, tok(tt)], in0=h[:, oc, tok(tt)],
                                                                                 in1=tbuf[:], op=ALU.add),
                         reads=[tkey, hk(oc, tt)], writes=[hk(oc, tt)])

    def rglru_block_slabs(n, lite):
        srcw = w_in.rearrange("(kc q) n -> q kc n", q=128)
        partsA = [
            (lambda t: t[:, 0:8192].rearrange("p (k n) -> p k n", k=16)[:, :, 0:256], srcw[:, :, n * 256:(n + 1) * 256]),
            (lambda t: t[:, 0:8192].rearrange("p (k n) -> p k n", k=16)[:, :, 256:512],
             srcw[:, :, 2048 + n * 256:2048 + (n + 1) * 256]),
        ]
        if lite:
            partsA = partsA[1:]
        sA = slab_load(partsA)
        sA["v"] = sA["t"][:, 0:8192].rearrange("p (k n) -> p k n", k=16)
        partsG = [
            (lambda t: t[:, 0:1024].rearrange("p (k n) -> p k n", k=2)[:, :, 0:256],
             w_ga[n].rearrange("(kc q) n -> q kc n", q=128)),
            (lambda t: t[:, 0:1024].rearrange("p (k n) -> p k n", k=2)[:, :, 256:512],
             w_gx[n].rearrange("(kc q) n -> q kc n", q=128)),
        ]
        sG = slab_load(partsG)
        sG["v"] = sG["t"][:, 0:1024].rearrange("p (k n) -> p k n", k=2)
        return sA, sG

    def rglru(lite=False, pre=None):
        rb = [rbuf, rstd[0]]
        rbk = ["rbuf", ("rstd", 0)]
        ib = [igbuf, rstd[1]]
        ibk = ["igbuf", ("rstd", 1)]
        XC = [[xcf[0], xcf[1]], [sqf[0], sqf[1]]]
        XCK = [[[("xcf", 0)], [("xcf", 1)]], [[("sq", 0), ("sq", 1)], [("sq", 2), ("sq", 3)]]]
        XB = [[xcb[0], xcb[1]], [ybuf[0], ybuf[1]]]
        XBK = [[("xcb", 0), ("xcb", 1)], [("ybuf", 0), ("ybuf", 1)]]
        slabsG = {}

        def stageA(i, n, tt, sA):
            par = i % 2
            xkeys = [xk(kc, tt) for kc in range(NCH)]
            for j in range(2):
                cc = 2 * n + j
                xm, xh = ("xrb_main", j), ("xrb_hist", j)
                xc, xck = XC[par][j], XCK[par][j]
                if not lite:
                    bank = gp_bank()
                    mm_group(ps[bank][:], bank, lambda kc: sA["v"][:, kc, j * 128:(j + 1) * 128],
                             lambda kc: xn[:, kc, tok(tt)], NCH, use(sA) + xkeys)
                    P.op(ACT, lambda e: e.activation(out=act2[:, cc, tok(tt)], in_=ps[bank][:], func=AF.Gelu_apprx_tanh),
                         reads=[("ps", bank)], writes=[ak(cc, tt)])
                bank2 = gp_bank()
                mm_group(ps[bank2][:], bank2, lambda kc: sA["v"][:, kc, 256 + j * 128:256 + (j + 1) * 128],
                         lambda kc: xn[:, kc, tok(tt)], NCH, use(sA) + xkeys)
                P.op(ACT, lambda e: e.activation(out=xrb[j][:, 3:3 + TW], in_=ps[bank2][:], func=AF.Copy),
                     reads=[("ps", bank2)], writes=[xm])
                P.op(DVE, lambda e: e.tensor_copy(out=xrb[j][:, 0:3], in_=hist[:, cc, 0:3]),
                     reads=[("hist", cc)], writes=[xh])
                cw = lambda jj, cc=cc: vecs[:, CONVW + 16 * jj + cc:CONVW + 16 * jj + cc + 1]
                P.op(ACT, lambda e: e.activation(out=xc[:], in_=ps[bank2][:], func=AF.Identity,
                                                 bias=vecs[:, CONVB + cc:CONVB + cc + 1], scale=cw(3)),
                     reads=[("ps", bank2), "vecs"], writes=xck)
                for jj in (2, 1, 0):
                    P.op(DVE, lambda e, jj=jj: e.scalar_tensor_tensor(out=xc[:], in0=xrb[j][:, jj:jj + TW], scalar=cw(jj),
                                                                      in1=xc[:], op0=ALU.mult, op1=ALU.add),
                         reads=[xm, xh, "vecs"] + xck, writes=xck)
                P.op(DVE, lambda e: e.tensor_copy(out=hist[:, cc, 0:3], in_=xrb[j][:, TW:TW + 3]),
                     reads=[xm], writes=[("hist", cc)])

        def stageA2(i):
            par = i % 2
            for j in range(2):
                P.op(ACT, lambda e: e.activation(out=XB[par][j][:], in_=XC[par][j][:], func=AF.Copy),
                     reads=XCK[par][j], writes=[XBK[par][j]])

        def stageB(i, n, tt, sG):
            par = i % 2
            brs, bis = [], []
            for j in range(2):
                br = gp_bank()
                mm_group(ps[br][:], br, lambda kc: sG["v"][:, kc, j * 128:(j + 1) * 128],
                         lambda kc: XB[par][kc][:], 2, use(sG) + XBK[par])
                bi = gp_bank()
                mm_group(ps[bi][:], bi, lambda kc: sG["v"][:, kc, 256 + j * 128:256 + (j + 1) * 128],
                         lambda kc: XB[par][kc][:], 2, use(sG) + XBK[par])
                brs.append(br)
                bis.append(bi)
            for j in range(2):
                cc = 2 * n + j
                P.op(ACT, lambda e: e.activation(out=rb[j][:], in_=ps[brs[j]][:], func=AF.Tanh,
                                                 bias=hb[:, cc:cc + 1], scale=0.5),
                     reads=[("ps", brs[j]), "hb"], writes=[rbk[j]])
                P.op(ACT, lambda e: e.activation(out=ib[j][:], in_=ps[bis[j]][:], func=AF.Tanh,
                                                 bias=hb[:, NCH + cc:NCH + cc + 1], scale=0.5),
                     reads=[("ps", bis[j]), "hb"], writes=[ibk[j]])
            for j in range(2):
                cc = 2 * n + j
                xc, xck = XC[par][j], XCK[par][j]
                P.op(ACT, lambda e: e.activation(out=rb[j][:], in_=rb[j][:], func=AF.Exp,
                                                 bias=c8h[:, cc:cc + 1], scale=c8h[:, cc:cc + 1]),
                     reads=[rbk[j], "c8h"], writes=[rbk[j]])
                P.op(DVE, lambda e: e.scalar_tensor_tensor(out=ib[j][:], in0=ib[j][:], scalar=1.0, in1=xc[:],
                                                           op0=ALU.add, op1=ALU.mult),
                     reads=[ibk[j]] + xck, writes=[ibk[j]])
            for j in range(2):
                xc, xck = XC[par][j], XCK[par][j]
                P.op(ACT, lambda e: e.activation(out=xc[:], in_=rb[j][:], func=AF.Square),
                     reads=[rbk[j]], writes=xck)
            for j in range(2):
                xc, xck = XC[par][j], XCK[par][j]
                P.op(ACT, lambda e: e.activation(out=xc[:], in_=xc[:], func=AF.Sqrt, bias=0.25, scale=-0.25),
                     reads=xck, writes=xck)
            for j in range(2):
                cc = 2 * n + j
                xc, xck = XC[par][j], XCK[par][j]
                P.op(DVE, lambda e: e.tensor_tensor(out=ib[j][:], in0=ib[j][:], in1=xc[:], op=ALU.mult),
                     reads=[ibk[j]] + xck, writes=[ibk[j]])
                P.op(DVE, lambda e: e.tensor_tensor_scan(out=xc[:], data0=rb[j][:], data1=ib[j][:],
                                                         initial=state[:, cc:cc + 1], op0=ALU.mult, op1=ALU.add),
                     reads=[rbk[j], ibk[j], ("state", cc)] + xck, writes=xck)
                P.op(DVE, lambda e: e.tensor_copy(out=state[:, cc:cc + 1], in_=xc[:, TW - 1:TW]),
                     reads=xck, writes=[("state", cc)])
                if not lite:
                    P.op(DVE, lambda e: e.tensor_tensor(out=act2[:, cc, tok(tt)], in0=xc[:], in1=act2[:, cc, tok(tt)], op=ALU.mult),
                         reads=xck + [ak(cc, tt)], writes=[ak(cc, tt)])

        steps = [(n, tt) for n in range(8) for tt in range(NT)]
        prev = None
        sA = None
        for i, (n, tt) in enumerate(steps):
            if tt == 0:
                if n == 0 and pre is not None:
                    sA, sG = pre
                else:
                    sA, sG = rglru_block_slabs(n, lite)
                slabsG[n] = sG
            stageA(i, n, tt, sA)
            if prev is not None:
                stageB(prev[0], prev[1], prev[2], slabsG[prev[1]])
            stageA2(i)
            prev = (i, n, tt)
        stageB(prev[0], prev[1], prev[2], slabsG[prev[1]])

    out_keys = []

    def layer0(p_own):
        pos = p_own * T
        attention(p_own, prefetched=True)
        load_h(pos)
        proj_add_h(w_oa, act2, ak)
        mlp(0)
        ple(0, p_own)

    def layer1(p_own, norm_done=False, pre=None):
        if not norm_done:
            rmsnorm_h(G_MIX[1])
        rglru(pre=pre)
        proj_add_h(w_or, act2, ak)
        mlp(1)
        ple(1, p_own)

    def store_out(p_own):
        pos = p_own * T
        for i in range(4):
            k = ("out", p_own, i)
            out_keys.append(k)
            P.dma(SP, lambda e, i=i, pos=pos: e.dma_start(
                out=outT.rearrange("(c q) n -> q c n", q=128)[:, 4 * i:4 * i + 4, pos:pos + T], in_=h[:, 4 * i:4 * i + 4, :]),
                reads=[hk(c, tt) for c in range(4 * i, 4 * i + 4) for tt in range(NT)], writes=[k])

    if True:
        rgp = [[2 * i, 2 * i + 1] for i in range(n_cores // 2)]
        load_h(0)
        for p in range(NPASS):
            rmsnorm_h(G_MIX[0])
            if p + 1 < NPASS:
                load_h((p + 1) * T)
            qkv_phase(p, rgp, late=((lambda: att_prefetch(0)) if p == NPASS - 1 else None))
        for p in range(NPASS):
            layer0(p)
            for i in range(4):
                P.dma(SP, lambda e, i=i, p=p: e.dma_start(
                    out=Hscr[p].rearrange("(c q) n -> q c n", q=128)[:, 4 * i:4 * i + 4, :], in_=h[:, 4 * i:4 * i + 4, :]),
                    reads=[hk(c, tt) for c in range(4 * i, 4 * i + 4) for tt in range(NT)], writes=[("Hscr", p, i)])
            rmsnorm_h(G_MIX[1])
            if p == NPASS - 1:
                load_h(0, src=Hscr[0], reads=[("Hscr", 0, i) for i in range(4)])
            else:
                att_prefetch(p + 1)
            rglru(lite=True)
        pre0 = rglru_block_slabs(0, False)
        P.op(DVE, lambda e: e.tensor_copy(out=carry[:, 0:16], in_=state[:]),
             reads=[("state", cc) for cc in range(NCH)], writes=["carry_a"])
        P.op(DVE, lambda e: e.tensor_copy(out=carry[:, 16:80], in_=hist[:].rearrange("p c k -> p (c k)")),
             reads=[("hist", cc) for cc in range(NCH)], writes=["carry_b"])
        P.dma(SP, lambda e: e.dma_start(out=cbounce, in_=carry[:]), reads=["carry_a", "carry_b"], writes=["cbounce"])
        rg = [[2 * i, 2 * i + 1] for i in range(n_cores // 2)]
        P.cc(POOL, lambda e: e.collective_compute("AllGather", ALU.bypass, replica_groups=rg, ins=[cbounce], outs=[cgath]),
             reads=["cbounce"], writes=["cgath"])
        rmsnorm_h(G_MIX[1])
        P.dma(SP, lambda e: e.dma_start(out=carry_in[:], in_=cgath[0:128, :]), reads=["cgath"], writes=["carry_in"])
        P.op(DVE, lambda e: e.tensor_scalar(out=carry_in[:], in0=carry_in[:], scalar1=vecs[:, 228:229], scalar2=None, op0=ALU.mult),
             reads=["carry_in", "vecs"], writes=["carry_in"])
        P.op(DVE, lambda e: e.tensor_copy(out=state[:], in_=carry_in[:, 0:16]),
             reads=["carry_in"], writes=[("state", cc) for cc in range(NCH)])
        P.op(DVE, lambda e: e.tensor_copy(out=hist[:].rearrange("p c k -> p (c k)"), in_=carry_in[:, 16:80]),
             reads=["carry_in"], writes=[("hist", cc) for cc in range(NCH)])
        for p in range(NPASS):
            if p > 0:
                load_h(0, src=Hscr[p], reads=[("Hscr", p, i) for i in range(4)])
            layer1(p, norm_done=(p == 0), pre=(pre0 if p == 0 else None))
            store_out(p)
    P.emit(final_wait_keys=out_keys)
    return nc, P


def _fv(v):
    return np.ascontiguousarray(np.asarray(v, np.float32).reshape(NCH, 128).T)


def _consts():
    kl = np.arange(128, dtype=np.float32)[:, None]
    alibi = np.zeros((128, 256), np.float32)
    for hh in range(NHEAD):
        slope = 2.0 ** (-8.0 * (hh + 1) / NHEAD)
        for mi in range(32):
            alibi[:, hh * 32 + mi] = (slope * (kl[:, 0] - 128.0 * mi)).astype(np.float32)
    tri = (np.arange(128)[:, None] <= np.arange(128)[None, :]).astype(np.float32)
    return alibi, tri


def make_in_maps(S, batches, x, p, g_mix, g_mlp, g_ple, w_qkv, g_q, g_k, lam_q1, lam_k1, lam_q2,
                 lam_k2, g_sub, w_o_attn, w_in_rec, conv_w, conv_b, w_gate_a, b_gate_a,
                 w_gate_x, b_gate_x, lam_rec, w_o_rec, w_up, w_down, w_ple_proj, w_ple_gate, split=False):
    f = lambda a: np.ascontiguousarray(np.asarray(a, np.float32))
    vecs = np.zeros((128, NV), np.float32)
    vecs[:, 0:16] = _fv(g_mix[0]); vecs[:, 16:32] = _fv(g_mlp[0]); vecs[:, 32:48] = _fv(g_ple[0])
    vecs[:, 48:64] = _fv(g_mix[1]); vecs[:, 64:80] = _fv(g_mlp[1]); vecs[:, 80:96] = _fv(g_ple[1])
    for j in range(4):
        vecs[:, 96 + 16 * j:96 + 16 * (j + 1)] = _fv(conv_w[0, j])
    vecs[:, 160:176] = _fv(conv_b[0]); vecs[:, 176:192] = _fv(b_gate_a[0]); vecs[:, 192:208] = _fv(b_gate_x[0])
    vecs[:, 208:224] = _fv(lam_rec[0])
    vecs[:, 224] = np.asarray(g_q[0], np.float32); vecs[:, 225] = np.asarray(g_k[0], np.float32)
    vecs[:, 226:228] = np.asarray(g_sub[0], np.float32).reshape(2, 128).T
    lamv = np.concatenate([np.asarray(a[0], np.float32) for a in (lam_q1, lam_k1, lam_q2, lam_k2)])
    lamb = np.ascontiguousarray(np.broadcast_to(lamv[None, :], (128, 512)))
    alibi, tri = _consts()
    shared = {
        "vecs": vecs, "lamb": lamb, "alibi": alibi, "tri": tri,
        "w_qkv": f(w_qkv[0]), "w_o_attn": f(w_o_attn[0]), "w_in_rec": f(w_in_rec[0]),
        "w_gate_a": f(w_gate_a[0]), "w_gate_x": f(w_gate_x[0]), "w_o_rec": f(w_o_rec[0]),
        "w_up": f(w_up), "w_down": f(w_down), "w_ple_proj": f(w_ple_proj), "w_ple_gate": f(w_ple_gate),
    }
    maps = []
    for b in batches:
        for hf in range(2 if split else 1):
            m = dict(shared)
            t0 = hf * S
            m["xT"] = np.ascontiguousarray(np.asarray(x[b, t0:t0 + S], np.float32).T)
            m["pT"] = np.ascontiguousarray(np.asarray(p[:, b, t0:t0 + S], np.float32).transpose(0, 2, 1))
            oc = np.ones((128, 32), np.float32)
            vv = vecs.copy()
            vv[:, 228] = float(hf)
            m["vecs"] = vv
            m["alibic"] = alibi if (hf == 1 or not split) else (alibi - np.float32(30000.0)).astype(np.float32)
            m["onescol"] = oc
            maps.append(m)
    return maps


def kernel(**inputs):
    x = inputs["x"]
    B, S, _ = x.shape
    SH = S // 2
    n_cores = 2 * B
    nc, _P = build(SH, S_ctx=SH, n_cores=n_cores)
    maps = make_in_maps(SH, list(range(B)), split=True, **inputs)
    res = run_bass_kernel_spmd(nc, maps, core_ids=list(range(n_cores)))
    out = np.empty((B, S, D), np.float32)
    for b in range(B):
        for hf in range(2):
            out[b, hf * SH:(hf + 1) * SH] = np.asarray(res.results[2 * b + hf]["outT"], np.float32).T
    return out
```
